# Optimizing a Trainium2 kernel written in Bass

```python
import jax, jax.numpy as jnp
from jax import lax
import numpy as np

D_MODEL = 1024
BATCH = 8
SEQ = 4096
DEPTH = 2

N_META = 16
N_MIXERS = 2
RMS_EPS = 1e-6

MLA_HEADS = 8
QK_NOPE = 128
QK_ROPE = 64
V_HEAD = 128
Q_LORA = 384
KV_LORA = 256
MLA_WIDTH = MLA_HEADS * V_HEAD
ROPE_BASE = 10000.0
Q_BLOCK = 128
MASK_VALUE = -1e30

LRU_WIDTH = 1024
LRU_BLOCKS = 4
LRU_BLOCK = LRU_WIDTH // LRU_BLOCKS
CONV_WIDTH = 4
LRU_C = 8.0

kernel_name = 'mla_rglru_interleaved_hybrid'


def rmsnorm(x, g):
    xf = x.astype(jnp.float32)
    y = xf * lax.rsqrt(jnp.mean(xf * xf, axis=-1, keepdims=True) + RMS_EPS)
    return (y * g.astype(jnp.float32)).astype(x.dtype)


def rotate_half_split(x, cos, sin):
    x1, x2 = jnp.split(x, 2, axis=-1)
    return jnp.concatenate([x1 * cos - x2 * sin, x1 * sin + x2 * cos], axis=-1).astype(x.dtype)


def block_causal_attention(q_nope, q_rope, k_nope, k_rope, v):
    B, T, H, _ = q_nope.shape
    pad = (-T) % Q_BLOCK
    Tp = T + pad
    nb = Tp // Q_BLOCK

    def padt(a):
        return jnp.pad(a, [(0, 0), (pad, 0)] + [(0, 0)] * (a.ndim - 2))

    q_nope, q_rope, k_nope, k_rope, v = (padt(a) for a in (q_nope, q_rope, k_nope, k_rope, v))
    scale = (QK_NOPE + QK_ROPE) ** -0.5
    key_idx = jnp.arange(Tp)

    def to_blocks(a):
        return jnp.moveaxis(a.reshape(B, nb, Q_BLOCK, *a.shape[2:]), 1, 0)

    def one_block(args):
        blk, qn, qr = args
        s = (jnp.einsum('bqhd,bkhd->bhqk', qn, k_nope, preferred_element_type=jnp.float32)
             + jnp.einsum('bqhr,bkr->bhqk', qr, k_rope, preferred_element_type=jnp.float32)) * scale
        q_idx = blk * Q_BLOCK + jnp.arange(Q_BLOCK)
        mask = (key_idx[None, :] <= q_idx[:, None]) & (key_idx[None, :] >= pad)
        s = jnp.where(mask[None, None], s, MASK_VALUE)
        p = jax.nn.softmax(s, axis=-1)
        return jnp.einsum('bhqk,bkhd->bqhd', p.astype(v.dtype), v)

    out = lax.map(one_block, (jnp.arange(nb), to_blocks(q_nope), to_blocks(q_rope)))
    out = jnp.moveaxis(out, 0, 1).reshape(B, Tp, H, V_HEAD)
    return out[:, pad:]


def mla_mixer(h, w_in, q_norm_g, kv_norm_g, w_uq, w_ukv, w_out):
    B, T, _ = h.shape
    proj = h @ w_in
    q_lat, kv_lat, k_rope, gate = jnp.split(
        proj, [Q_LORA, Q_LORA + KV_LORA, Q_LORA + KV_LORA + QK_ROPE], axis=-1)
    q = (rmsnorm(q_lat, q_norm_g) @ w_uq).reshape(B, T, MLA_HEADS, QK_NOPE + QK_ROPE)
    q_nope, q_rope = q[..., :QK_NOPE], q[..., QK_NOPE:]
    kv = (rmsnorm(kv_lat, kv_norm_g) @ w_ukv).reshape(B, T, MLA_HEADS, QK_NOPE + V_HEAD)
    k_nope, v = kv[..., :QK_NOPE], kv[..., QK_NOPE:]
    pos = jnp.arange(T, dtype=jnp.float32)
    inv_freq = ROPE_BASE ** (-jnp.arange(0, QK_ROPE, 2, dtype=jnp.float32) / QK_ROPE)
    ang = pos[:, None] * inv_freq[None, :]
    cos, sin = jnp.cos(ang), jnp.sin(ang)
    q_rope = rotate_half_split(q_rope, cos[:, None, :], sin[:, None, :])
    k_rope = rotate_half_split(k_rope, cos, sin)
    attn = block_causal_attention(q_nope, q_rope, k_nope, k_rope, v)
    y = attn.reshape(B, T, MLA_WIDTH) * jax.nn.silu(gate)
    return y @ w_out


def rglru_mixer(h, w_in, conv_w, conv_b, w_rg, b_rg, w_ig, b_ig, lam, w_out):
    B, T, _ = h.shape
    proj = h @ w_in
    u, gate = jnp.split(proj, [LRU_WIDTH], axis=-1)
    up = jnp.pad(u, ((0, 0), (CONV_WIDTH - 1, 0), (0, 0)))
    uc = conv_b + up[:, 0:T] * conv_w[0]
    for j in range(1, CONV_WIDTH):
        uc = uc + up[:, j:j + T] * conv_w[j]
    ub = uc.reshape(B, T, LRU_BLOCKS, LRU_BLOCK)
    r = jax.nn.sigmoid(jnp.einsum('btgi,gij->btgj', ub, w_rg).reshape(B, T, LRU_WIDTH) + b_rg)
    i = jax.nn.sigmoid(jnp.einsum('btgi,gij->btgj', ub, w_ig).reshape(B, T, LRU_WIDTH) + b_ig)
    log_a = -LRU_C * r.astype(jnp.float32) * jax.nn.softplus(-lam.astype(jnp.float32))
    a = jnp.exp(log_a)
    mult = jnp.sqrt(-jnp.expm1(2.0 * log_a))
    mult = jnp.where(jnp.arange(T)[None, :, None] == 0, 1.0, mult)
    b = mult * (i * uc).astype(jnp.float32)

    def combine(left, right):
        a1, b1 = left
        a2, b2 = right
        return a1 * a2, a2 * b1 + b2

    _, hs = lax.associative_scan(combine, (a, b), axis=1)
    y = hs.astype(h.dtype) * jax.nn.silu(gate)
    return y @ w_out


def setup_inputs(seed: int = 0) -> dict:
    key = jax.random.key(seed)
    ks = jax.random.split(key, 24)
    n_a = (DEPTH + 1) // 2
    n_b = DEPTH // 2
    d = D_MODEL
    f32 = jnp.float32

    def nrm(k, shape, fan_in):
        return jax.random.normal(k, shape, f32) * (fan_in ** -0.5)

    def gain(k, shape):
        return 1.0 + 0.01 * jax.random.normal(k, shape, f32)

    a_in_cols = Q_LORA + KV_LORA + QK_ROPE + MLA_WIDTH
    u0 = jax.random.uniform(ks[17], (n_b, LRU_WIDTH), f32, minval=0.9, maxval=0.999)
    s0 = u0 ** (1.0 / LRU_C)
    lam = jnp.log(s0) - jnp.log1p(-s0)
    return {
        'x': jax.random.normal(ks[0], (BATCH, SEQ, d), f32),
        'meta_tokens': jax.random.normal(ks[1], (N_META, d), f32),
        'a_norm_g': gain(ks[2], (n_a, d)),
        'a_w_in': nrm(ks[3], (n_a, d, a_in_cols), d),
        'a_q_norm_g': gain(ks[4], (n_a, Q_LORA)),
        'a_kv_norm_g': gain(ks[5], (n_a, KV_LORA)),
        'a_w_uq': nrm(ks[6], (n_a, Q_LORA, MLA_HEADS * (QK_NOPE + QK_ROPE)), Q_LORA),
        'a_w_ukv': nrm(ks[7], (n_a, KV_LORA, MLA_HEADS * (QK_NOPE + V_HEAD)), KV_LORA),
        'a_w_out': nrm(ks[8], (n_a, MLA_WIDTH, d), MLA_WIDTH),
        'b_norm_g': gain(ks[9], (n_b, d)),
        'b_w_in': nrm(ks[10], (n_b, d, 2 * LRU_WIDTH), d),
        'b_conv_w': nrm(ks[11], (n_b, CONV_WIDTH, LRU_WIDTH), CONV_WIDTH),
        'b_conv_b': 0.01 * jax.random.normal(ks[12], (n_b, LRU_WIDTH), f32),
        'b_w_rg': nrm(ks[13], (n_b, LRU_BLOCKS, LRU_BLOCK, LRU_BLOCK), LRU_BLOCK),
        'b_b_rg': 0.01 * jax.random.normal(ks[14], (n_b, LRU_WIDTH), f32),
        'b_w_ig': nrm(ks[15], (n_b, LRU_BLOCKS, LRU_BLOCK, LRU_BLOCK), LRU_BLOCK),
        'b_b_ig': 0.01 * jax.random.normal(ks[16], (n_b, LRU_WIDTH), f32),
        'b_lam': lam,
        'b_w_out': nrm(ks[18], (n_b, LRU_WIDTH, d), LRU_WIDTH),
        'final_norm_g': gain(ks[19], (d,)),
    }


def reference(x, meta_tokens, a_norm_g, a_w_in, a_q_norm_g, a_kv_norm_g, a_w_uq, a_w_ukv,
              a_w_out, b_norm_g, b_w_in, b_conv_w, b_conv_b, b_w_rg, b_b_rg, b_w_ig, b_b_ig,
              b_lam, b_w_out, final_norm_g):
    B = x.shape[0]
    meta = jnp.broadcast_to(meta_tokens[None].astype(x.dtype), (B, N_META, x.shape[-1]))
    h = jnp.concatenate([meta, x], axis=1)
    for layer in range(DEPTH):
        j = layer // N_MIXERS
        if layer % N_MIXERS == 0:
            h = h + mla_mixer(rmsnorm(h, a_norm_g[j]), a_w_in[j], a_q_norm_g[j], a_kv_norm_g[j],
                              a_w_uq[j], a_w_ukv[j], a_w_out[j])
        else:
            h = h + rglru_mixer(rmsnorm(h, b_norm_g[j]), b_w_in[j], b_conv_w[j], b_conv_b[j],
                                b_w_rg[j], b_b_rg[j], b_w_ig[j], b_b_ig[j], b_lam[j], b_w_out[j])
    h = rmsnorm(h, final_norm_g)
    return h[:, N_META:]
```

```python
import numpy as np
import ml_dtypes
from contextlib import ExitStack
import concourse.bass as bass
import concourse.mybir as mybir
from concourse.bass_utils import run_bass_kernel_spmd

F32 = mybir.dt.float32
BF16 = mybir.dt.bfloat16
AF = mybir.ActivationFunctionType
ALU = mybir.AluOpType

D = 1024
NMETA = 16
SEQ = 4096
CH = 512
EPS = 1e-6
NH = 8
QL = 384
KVL = 256
ROPE = 64
SCALE = float((128 + 64) ** -0.5)
NCORES = 8


class Prog:
    NDMA = 8

    def __init__(self, nc, stack, tag):
        stack = nc._semstack
        self.nc = nc
        self.tag = tag
        self.names = ['pe', 'act', 'dve', 'pool', 'sp']
        self.ops = {e: [] for e in self.names}
        self.sem = {}
        for e in ('pe', 'act', 'dve', 'pool'):
            self.sem[e] = (f'{tag}_s_{e}', stack.enter_context(nc.semaphore(f'{tag}_s_{e}')))
        self.dsem = {}
        self.dcnt = {}
        self.stack = stack
        self.lastw = {}
        self.readers = {}

    def _dsems(self, q):
        if q not in self.dsem:
            self.dsem[q] = [(f'{self.tag}_d_{q}{i}',
                             self.stack.enter_context(self.nc.semaphore(f'{self.tag}_d_{q}{i}')))
                            for i in range(self.NDMA)]
            self.dcnt[q] = 0
        return self.dsem[q]

    def op(self, eng, fn, reads=(), writes=(), dma=False):
        idx = len(self.ops[eng])
        deps = set()

        def add(prod, kind):
            if prod is None:
                return
            peng, pidx = prod
            if peng == eng and pidx == idx:
                return
            pop = self.ops[peng][pidx]
            if peng == eng and not pop['dma'] and not dma:
                if eng == 'pe':
                    return
            deps.add(prod)
            pop['signal'] = True

        for k in reads:
            add(self.lastw.get(k), 'RAW')
            if k.startswith('ps'):
                for r in self.readers.get(k, ()):
                    if r[0] != eng:
                        add(r, 'WAR')
        for k in writes:
            add(self.lastw.get(k), 'WAW')
            for r in self.readers.get(k, ()):
                add(r, 'WAR')
        for k in reads:
            if k not in writes:
                self.readers.setdefault(k, []).append((eng, idx))
        for k in writes:
            self.lastw[k] = (eng, idx)
            self.readers[k] = []
        o = dict(fn=fn, deps=deps, dma=dma, signal=False)
        if dma:
            sems = self._dsems(eng)
            i = self.dcnt[eng]
            self.dcnt[eng] += 1
            o['dsem'] = sems[i % self.NDMA]
            o['dval'] = 16 * (i // self.NDMA + 1)
            o['prev'] = (sems[i % self.NDMA], 16 * (i // self.NDMA)) if i >= self.NDMA else None
        self.ops[eng].append(o)

    def mm(self, out, lhsT, rhs, start, stop, r, w):
        self.op('pe', lambda e: e.matmul(out, lhsT, rhs, start=start, stop=stop), reads=r, writes=w)

    def tr(self, out, in_, ident, r, w):
        self.op('pe', lambda e: e.transpose(out, in_, ident), reads=r, writes=w)

    def act(self, out, in_, func, r, w, **kw):
        self.op('act', lambda e: e.activation(out, in_, func, **kw), reads=r, writes=w)

    def cp(self, eng, out, in_, r, w):
        if eng == 'act':
            self.op('act', lambda e: e.copy(out, in_), reads=r, writes=w)
        else:
            self.op(eng, lambda e: e.tensor_copy(out, in_), reads=r, writes=w)

    def tt(self, eng, out, in0, in1, op, r, w):
        self.op(eng, lambda e: e.tensor_tensor(out, in0, in1, op), reads=r, writes=w)

    def ts(self, eng, out, in0, s1, s2, op0, op1, r, w):
        if op1 is None:
            self.op(eng, lambda e: e.tensor_scalar(out, in0, s1, None, op0), reads=r, writes=w)
        else:
            self.op(eng, lambda e: e.tensor_scalar(out, in0, s1, s2, op0, op1), reads=r, writes=w)

    def cvt(self, eng, out, in_, g, gneg, neg, r, w):
        if eng == 'act':
            if g is None:
                self.op('act', lambda e: e.copy(out, in_), reads=r, writes=w)
            else:
                sc = gneg if neg else g
                self.op('act', lambda e: e.activation(out, in_, AF.Identity, scale=sc), reads=r, writes=w)
        else:
            if g is None:
                self.op(eng, lambda e: e.tensor_copy(out, in_), reads=r, writes=w)
            elif neg:
                self.op(eng, lambda e: e.tensor_scalar(out, in_, g, -1.0, ALU.mult, ALU.mult), reads=r, writes=w)
            else:
                self.op(eng, lambda e: e.tensor_scalar(out, in_, g, None, ALU.mult), reads=r, writes=w)

    def stt(self, out, in0, scalar, in1, op0, op1, r, w):
        self.op('dve', lambda e: e.scalar_tensor_tensor(out, in0, scalar, in1, op0, op1), reads=r, writes=w)

    def dma(self, out, in_, r, w, q='sp', slow=False):
        if slow:
            def f(e):
                with self.nc.allow_non_contiguous_dma(reason="tiny strided vector load"):
                    return e.dma_start(out=out, in_=in_)
            self.op(q, f, reads=r, writes=w, dma=True)
        else:
            self.op(q, lambda e: e.dma_start(out=out, in_=in_), reads=r, writes=w, dma=True)

    def emit(self):
        nc = self.nc
        for e in self.names:
            cnt = 0
            for o in self.ops[e]:
                if not o['dma'] and o['signal']:
                    cnt += 1
                    o['sigval'] = cnt

        def run(e, E):
            waited = {}
            for o in self.ops[e]:
                waits = []
                for (peng, pidx) in o['deps']:
                    pop = self.ops[peng][pidx]
                    if pop['dma']:
                        waits.append((pop['dsem'], pop['dval']))
                    else:
                        waits.append((self.sem[peng], pop['sigval']))
                if o['dma'] and o['prev'] is not None:
                    waits.append(o['prev'])
                for (nm, s), v in sorted(waits, key=lambda t: (t[0][0], t[1])):
                    if waited.get(nm, 0) >= v:
                        continue
                    E.wait_ge(s, v)
                    waited[nm] = v
                ins = o['fn'](E)
                if o['dma']:
                    ins.then_inc(o['dsem'][1], 16)
                elif o['signal']:
                    ins.then_inc(self.sem[e][1], 1)
            if e in self.dsem:
                n = self.dcnt[e]
                for i, (nm, s) in enumerate(self.dsem[e]):
                    k = (n - i + self.NDMA - 1) // self.NDMA if n > i else 0
                    if k > 0 and waited.get(nm, 0) < 16 * k:
                        E.wait_ge(s, 16 * k)

        with nc.Block() as block:
            if self.ops['pe']:
                @block.tensor
                def _(E):
                    run('pe', E)
            if self.ops['act']:
                @block.scalar
                def _(E):
                    run('act', E)
            if self.ops['dve']:
                @block.vector
                def _(E):
                    run('dve', E)
            if self.ops['pool']:
                @block.gpsimd
                def _(E):
                    run('pool', E)
            if self.ops['sp']:
                @block.sync
                def _(E):
                    run('sp', E)


def chunk_list(nreal):
    chunks = [(0, NMETA, [(0, NMETA)])]
    for c in range(nreal):
        chunks.append((NMETA + CH * c, CH, [(128 * j, 128) for j in range(CH // 128)]))
    return chunks


class Ctx:
    pass


def alloc_common(nc, st, X, tag, stage_cols=2048):
    sb = lambda name, shape, dt: st.enter_context(nc.sbuf_tensor(f'{tag}_{name}', shape, dt))
    X.ident = sb('ident', [128, 128], BF16)
    X.hbuf = sb('hbuf', [128, 4, D], F32)
    X.tokbf = [sb(f'tokbf{i}', [128, D], BF16) for i in range(2)]
    X.hnT = sb('hnT', [128, 8, CH], BF16)
    X.gT = sb('gT', [128, 8, CH], BF16)
    X.ssq = sb('ssq', [128, 4], F32)
    X.srt = sb('srt', [128, 4], F32)
    X.rstd = sb('rstd', [128, 4], F32)
    X.stage = sb('stage', [128, stage_cols], F32)
    X.ps = [st.enter_context(nc.psum_tensor(f'{tag}_psb{i}', [128, 512], F32)) for i in range(8)]
    X.ptr = [X.ps[7][:, :].bitcast(BF16).rearrange('p (a b) -> p a b', b=128)]
    X.sb = sb


def load_chunk(P, X, tiles, src_rows, hb=None, hk='hbuf'):
    hb = X.hbuf if hb is None else hb
    for j, (off, tn) in enumerate(tiles):
        P.dma(hb[0:tn, j, :], src_rows(j), r=[], w=[f'{hk}{j}'])


def norm_transpose(P, X, tiles, tnmax, hb=None, hk='hbuf'):
    hb = X.hbuf if hb is None else hb
    nt_ = len(tiles)
    for j, (off, tn) in enumerate(tiles):
        P.act(X.tokbf[j % 2][0:tn, :], hb[0:tn, j, :], AF.Square, r=[f'{hk}{j}'], w=[f'tokbf{j % 2}', f'ssq{j}'],
              accum_out=X.ssq[0:tn, j:j + 1])
    P.act(X.srt[0:tnmax, 0:nt_], X.ssq[0:tnmax, 0:nt_], AF.Sqrt, r=[f'ssq{j}' for j in range(nt_)] + ['epsc'],
          w=['srt'], scale=1.0 / D, bias=X.epsc[0:tnmax, 0:1])
    P.op('dve', lambda e: e.reciprocal(X.rstd[0:tnmax, 0:nt_], X.srt[0:tnmax, 0:nt_]), reads=['srt'], writes=['rstd'])
    for j, (off, tn) in enumerate(tiles):
        tb = X.tokbf[j % 2]
        tk = f'tokbf{j % 2}'
        P.ts('dve', tb[0:tn, :], hb[0:tn, j, :], X.rstd[0:tn, j:j + 1], None, ALU.mult, None,
             r=[f'{hk}{j}', 'rstd'], w=[tk])
        pt = X.ptr[0]
        pk = 'ps7'
        for k in range(8):
            P.tr(pt[:, k, 0:tn], tb[0:tn, k * 128:(k + 1) * 128], X.ident[0:tn, 0:tn], r=[tk, 'ident'], w=[pk])
        P.cp('act' if j % 2 else 'dve', X.hnT[:, :, off:off + tn], pt[:, :, 0:tn], r=[pk], w=['hnT'])


def wout_residual(P, X, tiles, W, wkey, hb=None, hk='hbuf', gT=None, gk='gT'):
    hb = X.hbuf if hb is None else hb
    gT = X.gT if gT is None else gT
    n = 0
    for j, (off, tn) in enumerate(tiles):
        for half in range(2):
            b = n % 2
            n += 1
            pk = f'ps{b}'
            for h in range(8):
                P.mm(X.ps[b][0:tn, :], gT[:, h, off:off + tn], W[:, h, half * 512:(half + 1) * 512],
                     h == 0, h == 7, r=[gk, wkey], w=[pk])
            P.tt('dve', hb[0:tn, j, half * 512:(half + 1) * 512], hb[0:tn, j, half * 512:(half + 1) * 512],
                 X.ps[b][0:tn, :], ALU.add, r=[pk, f'{hk}{j}'], w=[f'{hk}{j}'])


def load_consts(P, X, ident_d):
    P.dma(X.ident[:, :], ident_d, r=[], w=['ident'])
    P.op('pool', lambda e: e.memset(X.epsc[:, :], EPS), writes=['epsc'])


def build_phase1(nc, A, nreal):
    with ExitStack() as st:
        X = Ctx()
        alloc_common(nc, st, X, 'sA')
        sb = X.sb
        X.epsc = sb('epsc', [128, 1], F32)
        tri = sb('tri', [128, 128], BF16)
        ones_bf = sb('ones_bf', [128, 128], BF16)
        ones_f = sb('ones_f', [128, 128], F32)
        ga = sb('ga', [128, 8], F32)
        gq = sb('gq', [128, 3], F32)
        gkv = sb('gkv', [128, 2], F32)
        WinA = sb('WinA', [128, 8, 1984], BF16)
        Wuq = sb('Wuq', [128, 3, 8, 256], BF16)
        WukT = sb('WukT', [128, 8, 256], BF16)
        Wuv = sb('Wuv', [128, 2, 8, 128], BF16)
        WoutA = sb('WoutA', [128, 8, D], BF16)
        TT = NMETA + CH * nreal
        NKT = 1 + 4 * nreal
        ckvT = sb('ckvT', [128, 2, TT], BF16)
        ckv = sb('ckv', [128, NKT, 256], BF16)
        krT = sb('krT', [128, TT], BF16)
        sq = [sb(f'sq{i}', [128, CH], F32) for i in range(3)]
        sqk = [sb(f'sqk{i}', [128, CH], F32) for i in range(2)]
        rstdq = sb('rstdq', [128, CH], F32)
        rstdkv = sb('rstdkv', [128, CH], F32)
        stmp = sb('stmp', [128, CH], F32)
        qlatT = sb('qlatT', [128, 3, CH], BF16)
        cosb = sb('cosb', [128, CH], F32)
        sinb = sb('sinb', [128, CH], F32)
        csb = sb('csb', [128, CH], F32)
        csq = sb('csq', [128, CH], F32)
        rtmp = sb('rtmp', [128, CH], F32)
        qn = [sb(f'qn{i}', [128, CH], BF16) for i in range(2)]
        Qt = [sb(f'Qt{i}', [128, 2, CH], BF16) for i in range(2)]
        qrope = [sb(f'qrope{i}', [128, CH], BF16) for i in range(2)]
        pT = [sb(f'pT{i}', [128, CH], BF16) for i in range(4)]
        pTm = sb('pTm', [128, CH], BF16)
        olat = sb('olat', [128, 2, CH], BF16)
        rinv = sb('rinv', [128, CH], F32)
        ytmp = sb('ytmp', [128, CH], F32)
        racc = [sb(f'racc{i}', [128, CH], F32) for i in range(2)]
        P = Prog(nc, st, 'p1')
        ps = X.ps
        stage = X.stage

        load_consts(P, X, A['ident'])
        P.dma(tri[:, :], A['tri'], r=[], w=['tri'])
        P.op('pool', lambda e: e.memset(ones_bf[:, :], 1.0), writes=['ones_bf'])
        P.op('pool', lambda e: e.memset(ones_f[:, :], 1.0), writes=['ones_f'])
        P.op('pool', lambda e: e.memset(pTm[:, :], 0.0), writes=['pTm'])
        P.op('pool', lambda e: e.memset(ckvT[:, :, 0:128], 0.0), writes=['ckvT0', 'ckvT1'])
        P.op('pool', lambda e: e.memset(krT[:, 0:128], 0.0), writes=['krT0', 'krT1'])
        P.op('pool', lambda e: e.memset(ckv[:, 0, :], 0.0), writes=['ckv0'])
        P.dma(ga[:, :], A['a_norm_g'].rearrange('(k p) -> p k', p=128), r=[], w=['ga'], slow=True)
        P.dma(gq[:, :], A['a_q_norm_g'].rearrange('(k p) -> p k', p=128), r=[], w=['gq'], slow=True)
        P.dma(gkv[:, :], A['a_kv_norm_g'].rearrange('(k p) -> p k', p=128), r=[], w=['gkv'], slow=True)
        gan = sb('gan', [128, 8], F32)
        gqn = sb('gqn', [128, 3], F32)
        P.ts('dve', gan[:, :], ga[:, :], -1.0, None, ALU.mult, None, r=['ga'], w=['gan'])
        P.ts('dve', gqn[:, :], gq[:, :], -1.0, None, ALU.mult, None, r=['gq'], w=['gqn'])
        slots = [(stage[:, 0:2048], ['stage']),
                 (X.hbuf[:, 0:2, :].rearrange('p a b -> p (a b)'), ['hbuf0', 'hbuf1']),
                 (X.hbuf[:, 2:4, :].rearrange('p a b -> p (a b)'), ['hbuf2', 'hbuf3'])]
        nslab = [0]

        def next_slot():
            i = nslab[0]
            nslab[0] += 1
            return slots[i % 3][0], slots[i % 3][1], ('dve' if i % 2 == 0 else 'act')

        for k in range(8):
            sl, sk, eng = next_slot()
            P.dma(sl[:, 0:1728], A['a_w_in'][k * 128:(k + 1) * 128, :], r=[], w=sk)
            g, gn = ga[:, k:k + 1], gan[:, k:k + 1]
            rk_ = sk + ['ga', 'gan']
            P.cvt(eng, WinA[:, k, 0:1728], sl[:, 0:1728], g, gn, False, r=rk_, w=['WinA'])
            for dup in range(2):
                o = 1728 + 64 * dup
                P.cvt(eng, WinA[:, k, o:o + 64], sl[:, 640:704], g, gn, False, r=rk_, w=['WinA'])
                o = 1856 + 64 * dup
                P.cvt(eng, WinA[:, k, o:o + 32], sl[:, 672:704], g, gn, True, r=rk_, w=['WinA'])
                P.cvt(eng, WinA[:, k, o + 32:o + 64], sl[:, 640:672], g, gn, False, r=rk_, w=['WinA'])
        for k in range(3):
            sl, sk, eng = next_slot()
            P.dma(sl[:, 0:1536], A['a_w_uq'][k * 128:(k + 1) * 128, :], r=[], w=sk)
            g, gn = gq[:, k:k + 1], gqn[:, k:k + 1]
            rk_ = sk + ['gq', 'gqn']
            sv = sl[:, 0:1536].rearrange('p (h c) -> p h c', c=192)
            P.cvt(eng, Wuq[:, k, :, 0:192], sv, g, gn, False, r=rk_, w=['Wuq'])
            P.cvt(eng, Wuq[:, k, :, 192:224], sv[:, :, 160:192], g, gn, True, r=rk_, w=['Wuq'])
            P.cvt(eng, Wuq[:, k, :, 224:256], sv[:, :, 128:160], g, gn, False, r=rk_, w=['Wuq'])
        identf = sb('identf', [128, 128], F32)
        P.cp('dve', identf[:, :], X.ident[:, :], r=['ident'], w=['identf'])
        for lc in range(2):
            sl, sk, eng = next_slot()
            P.dma(sl[:, 0:2048], A['a_w_ukv'][lc * 128:(lc + 1) * 128, :], r=[], w=sk)
            sv = sl[:, 0:2048].rearrange('p (h c) -> p h c', c=256)
            P.cvt(eng, Wuv[:, lc, :, :], sv[:, :, 128:256], None, None, False, r=sk, w=['Wuv'])
            for h in range(NH):
                b = h % 2
                P.tr(ps[b][:, 0:128], sv[:, h, 0:128], identf[:, :], r=sk + ['identf'], w=[f'ps{b}'])
                P.cp('act', WukT[:, h, lc * 128:(lc + 1) * 128], ps[b][:, 0:128], r=[f'ps{b}'], w=['WukT'])
        for k2 in range(4):
            sl, sk, eng = next_slot()
            P.dma(sl[:, 0:2048].rearrange('p (k n) -> p k n', k=2),
                  A['a_w_out'][k2 * 256:(k2 + 1) * 256, :].rearrange('(k p) n -> p k n', p=128), r=[], w=sk)
            P.cvt(eng, WoutA[:, 2 * k2:2 * k2 + 2, :], sl[:, 0:2048].rearrange('p (k n) -> p k n', k=2), None, None, False, r=sk, w=['WoutA'])

        chunks = chunk_list(nreal)
        for c, (t0, nt, tiles) in enumerate(chunks):
            if c == 0:
                load_chunk(P, X, tiles, lambda j: A['meta'][0:NMETA, :])
                P.dma(cosb[:, 0:nt], A['cos'][:, t0:t0 + nt], r=[], w=['cosb'])
                P.dma(sinb[:, 0:nt], A['sin'][:, t0:t0 + nt], r=[], w=['sinb'])
                P.dma(csb[:, 0:nt], A['cs'][:, t0:t0 + nt], r=[], w=['csb'])
            norm_transpose(P, X, tiles, tiles[0][1])

            def proj(col0, M, b):
                for k in range(8):
                    P.mm(ps[b][0:M, 0:nt], WinA[:, k, col0:col0 + M], X.hnT[:, k, 0:nt], k == 0, k == 7,
                         r=['WinA', 'hnT'], w=[f'ps{b}'])

            for i in range(2):
                proj(QL + 128 * i, 128, 2 + i)
                P.act(sqk[i][:, 0:nt], ps[2 + i][:, 0:nt], AF.Square, r=[f'ps{2 + i}'], w=[f'sqk{i}'])
            for i in range(3):
                b = i % 2
                proj(128 * i, 128, b)
                P.act(sq[i][:, 0:nt], ps[b][:, 0:nt], AF.Square, r=[f'ps{b}'], w=[f'sq{i}'])
                P.cp('dve', qlatT[:, i, 0:nt], ps[b][:, 0:nt], r=[f'ps{b}'], w=['qlatT'])
            for i in range(2):
                P.mm(ps[4][:, 0:nt], ones_f[:, :], sqk[i][:, 0:nt], i == 0, i == 1, r=['ones_f', f'sqk{i}'], w=['ps4'])
            for i in range(3):
                P.mm(ps[5][:, 0:nt], ones_f[:, :], sq[i][:, 0:nt], i == 0, i == 2, r=['ones_f', f'sq{i}'], w=['ps5'])
            P.act(rstdkv[:, 0:nt], ps[4][:, 0:nt], AF.Ln, r=['ps4', 'epsc'], w=['rstdkv'], scale=1.0 / KVL, bias=X.epsc[:, 0:1])
            P.act(rstdkv[:, 0:nt], rstdkv[:, 0:nt], AF.Exp, r=['rstdkv'], w=['rstdkv'], scale=-0.5)
            P.act(rstdq[:, 0:nt], ps[5][:, 0:nt], AF.Ln, r=['ps5', 'epsc'], w=['rstdq'], scale=1.0 / QL, bias=X.epsc[:, 0:1])
            P.act(rstdq[:, 0:nt], rstdq[:, 0:nt], AF.Exp, r=['rstdq'], w=['rstdq'], scale=-0.5)

            def gate_groups(hs):
                for h in hs:
                    b = h % 2
                    proj(704 + 128 * h, 128, b)
                    P.act(X.gT[:, h, 0:nt], ps[b][:, 0:nt], AF.Silu, r=[f'ps{b}'], w=['gT'])

            gate_groups(range(0, 3))
            for i in range(2):
                P.stt(ckvT[:, i, t0:t0 + nt], ps[2 + i][:, 0:nt], gkv[:, i:i + 1], rstdkv[:, 0:nt], ALU.mult, ALU.mult,
                      r=[f'ps{2 + i}', 'gkv', 'rstdkv'], w=[f'ckvT{c}'])
            P.tt('pool', csq[:, 0:nt], csb[:, 0:nt], rstdq[:, 0:nt], ALU.mult, r=['csb', 'rstdq'], w=['csq'])
            proj(1728, 128, 0)
            proj(1856, 128, 1)
            P.tt('dve', rtmp[:, 0:nt], ps[0][:, 0:nt], cosb[:, 0:nt], ALU.mult, r=['ps0', 'cosb'], w=['rtmp'])
            P.tt('dve', stmp[:, 0:nt], ps[1][:, 0:nt], sinb[:, 0:nt], ALU.mult, r=['ps1', 'sinb'], w=['stmp'])
            P.tt('dve', krT[:, t0:t0 + nt], rtmp[:, 0:nt], stmp[:, 0:nt], ALU.add, r=['rtmp', 'stmp'], w=[f'krT{c}'])
            if c + 1 < len(chunks):
                nt0, nnt = chunks[c + 1][0], chunks[c + 1][1]
                P.dma(cosb[:, 0:nnt], A['cos'][:, nt0:nt0 + nnt], r=[], w=['cosb'])
                P.dma(sinb[:, 0:nnt], A['sin'][:, nt0:nt0 + nnt], r=[], w=['sinb'])
                P.dma(csb[:, 0:nnt], A['cs'][:, nt0:nt0 + nnt], r=[], w=['csb'])
            gate_groups(range(3, 6))
            for j, (off, tn) in enumerate(tiles):
                kt = 0 if c == 0 else 1 + 4 * (c - 1) + j
                pt = X.ptr[0]
                pk = 'ps7'
                for i in range(2):
                    P.tr(pt[0:tn, i, :], ckvT[:, i, t0 + off:t0 + off + tn], X.ident[:, :], r=[f'ckvT{c}', 'ident'], w=[pk])
                P.cp('act' if j % 2 else 'dve', ckv[0:tn, kt, :], pt[0:tn, 0:2, :], r=[pk], w=[f'ckv{kt}'])
            gate_groups(range(6, 8))

            units = []
            if c == 0:
                units.append((0, 0, NMETA, 0, True))
            else:
                units.append((0, 0, NMETA, 0, False))
                for cc in range(1, c):
                    for j in range(4):
                        units.append((1 + 4 * (cc - 1) + j, NMETA + CH * (cc - 1) + 128 * j, 128, 0, False))
                for j in range(4):
                    units.append((1 + 4 * (c - 1) + j, t0 + 128 * j, 128, 128 * j, True))

            SB = [2, 3, 7]

            def kchunk(k0):
                return 0 if k0 < NMETA else 1 + (k0 - NMETA) // CH

            ACC = [(4, 5), (6, 1)]

            def g_qn(h):
                s = h % 2
                for k in range(3):
                    P.mm(ps[0][:, 0:nt], Wuq[:, k, h, 0:128], qlatT[:, k, 0:nt], k == 0, k == 2, r=['Wuq', 'qlatT'], w=['ps0'])
                P.tt('dve', qn[s][:, 0:nt], ps[0][:, 0:nt], rstdq[:, 0:nt], ALU.mult, r=['ps0', 'rstdq'], w=[f'qn{s}'])

            def g_rope(h):
                s = h % 2
                for k in range(3):
                    P.mm(ps[0][:, 0:nt], Wuq[:, k, h, 128:256], qlatT[:, k, 0:nt], k == 0, k == 2, r=['Wuq', 'qlatT'], w=['ps0'])
                P.tt('dve', qrope[s][:, 0:nt], ps[0][:, 0:nt], csq[:, 0:nt], ALU.mult, r=['ps0', 'csq'], w=[f'qrope{s}'])

            def g_qt(h, lc):
                s = h % 2
                P.mm(ps[0][:, 0:nt], WukT[:, h, lc * 128:(lc + 1) * 128], qn[s][:, 0:nt], True, True, r=['WukT', f'qn{s}'], w=['ps0'])
                P.cp('dve', Qt[s][:, lc, 0:nt], ps[0][:, 0:nt], r=['ps0'], w=[f'Qt{s}'])

            def g_rowsum(h):
                P.mm(ps[0][:, 0:nt], ones_f[:, :], racc[h % 2][:, 0:nt], True, True, r=['ones_f', f'racc{h % 2}'], w=['ps0'])

            def g_rinv(h):
                P.act(rinv[:, 0:nt], ps[0][:, 0:nt], AF.Ln, r=['ps0'], w=['rinv'])
                P.act(rinv[:, 0:nt], rinv[:, 0:nt], AF.Exp, r=['rinv'], w=['rinv'], scale=-1.0)

            def g_y(h):
                for lc in range(2):
                    P.mm(ps[0][:, 0:nt], Wuv[:, lc, h, :], olat[:, lc, 0:nt], lc == 0, lc == 1, r=['Wuv', 'olat'], w=['ps0'])
                P.tt('dve', ytmp[:, 0:nt], ps[0][:, 0:nt], rinv[:, 0:nt], ALU.mult, r=['ps0', 'rinv'], w=['ytmp'])
                P.tt('pool', X.gT[:, h, 0:nt], ytmp[:, 0:nt], X.gT[:, h, 0:nt], ALU.mult, r=['ytmp', 'gT'], w=['gT'])

            def smm(h, u, ui):
                kt, k0, kn, qlo, diag = u
                s = h % 2
                b = SB[ui % 3]
                kc = kchunk(k0)
                km = 128
                rk = [f'ckvT{kc}', f'krT{kc}'] + (['ckvT1', 'krT1'] if kt == 0 else [])
                for lc in range(2):
                    P.mm(ps[b][0:km, qlo:nt], ckvT[:, lc, k0:k0 + km], Qt[s][:, lc, qlo:nt], lc == 0, False,
                         r=rk + [f'Qt{s}'], w=[f'ps{b}'])
                P.mm(ps[b][0:km, qlo:nt], krT[:, k0:k0 + km], qrope[s][:, qlo:nt], False, True,
                     r=rk + [f'qrope{s}'], w=[f'ps{b}'])

            def pv_exp(h, u, ui, nu):
                kt, k0, kn, qlo, diag = u
                b = SB[ui % 3]
                pb = pT[ui % 4]
                pk = f'pT{ui % 4}'
                if kt == 0:
                    pb, pk = pTm, 'pTm'
                P.act(pb[0:kn, qlo:nt], ps[b][0:kn, qlo:nt], AF.Exp, r=[f'ps{b}'], w=[pk], scale=SCALE)
                if diag:
                    P.tt('pool', pb[0:kn, qlo:qlo + kn], pb[0:kn, qlo:qlo + kn], tri[0:kn, 0:kn], ALU.mult, r=[pk, 'tri'], w=[pk])
                ra = racc[h % 2]
                rk = f'racc{h % 2}'
                if ui == 0:
                    P.op('pool', lambda e, ra=ra: e.memset(ra[:, :], 0.0), writes=[rk])
                P.tt('dve', ra[0:kn, qlo:nt], ra[0:kn, qlo:nt], pb[0:kn, qlo:nt], ALU.add, r=[rk, pk], w=[rk])

            def pv_mm(h, u, ui, nu):
                kt, k0, kn, qlo, diag = u
                pb = pT[ui % 4]
                pk = f'pT{ui % 4}'
                if kt == 0:
                    pb, pk = pTm, 'pTm'
                first = ui == 0
                last = ui == nu - 1
                for lc in range(2):
                    a = ACC[h % 2][lc]
                    P.mm(ps[a][:, qlo:nt], ckv[0:128, kt, lc * 128:(lc + 1) * 128], pb[0:128, qlo:nt], first, last,
                         r=[f'ckv{kt}', pk], w=[f'ps{a}'])

            def fin_evac(h):
                for lc in range(2):
                    a = ACC[h % 2][lc]
                    P.cp('dve', olat[:, lc, 0:nt], ps[a][:, 0:nt], r=[f'ps{a}'], w=['olat'])

            nu = len(units)
            for g in (g_qn, g_rope):
                g(0)
            g_qt(0, 0)
            g_qt(0, 1)
            for ui in range(min(2, nu)):
                smm(0, units[ui], ui)
            for h in range(NH):
                misc = []
                if h >= 1:
                    misc += [lambda h=h: g_rowsum(h - 1), lambda h=h: g_rinv(h - 1), lambda h=h: g_y(h - 1)]
                if h + 1 < NH:
                    misc += [lambda h=h: g_qn(h + 1), lambda h=h: g_rope(h + 1),
                             lambda h=h: g_qt(h + 1, 0), lambda h=h: g_qt(h + 1, 1)]
                for ui in range(nu):
                    if ui + 2 < nu:
                        smm(h, units[ui + 2], ui + 2)
                    pv_exp(h, units[ui], ui, nu)
                    pv_mm(h, units[ui], ui, nu)
                    take = 1 if nu - ui > len(misc) else (len(misc) if ui == nu - 1 else 2)
                    for _ in range(min(take, len(misc))):
                        misc.pop(0)()
                while misc:
                    misc.pop(0)()
                if h + 1 < NH:
                    for ui in range(min(2, nu)):
                        smm(h + 1, units[ui], ui)
                fin_evac(h)
            g_rowsum(NH - 1)
            g_rinv(NH - 1)
            g_y(NH - 1)

            wout_residual(P, X, tiles, WoutA, 'WoutA')
            for j, (off, tn) in enumerate(tiles):
                P.dma(A['h1'][t0 + off:t0 + off + tn, :], X.hbuf[0:tn, j, :], r=[f'hbuf{j}'], w=[])
                if c + 1 < len(chunks) and c > 0:
                    P.dma(X.hbuf[:, j, :], A['x'][CH * c + 128 * j: CH * c + 128 * (j + 1), :], r=[], w=[f'hbuf{j}'], q='act')
            if c == 0 and len(chunks) > 1:
                for j in range(4):
                    P.dma(X.hbuf[:, j, :], A['x'][128 * j: 128 * (j + 1), :], r=[], w=[f'hbuf{j}'])
        P.emit()


def build_phase2(nc, A, nreal):
    with ExitStack() as st:
        X = Ctx()
        alloc_common(nc, st, X, 'sB', stage_cols=1024)
        sb = X.sb
        X.epsc = sb('epsc', [128, 1], F32)
        hbufs = [X.hbuf, sb('hbufB', [128, 4, D], F32), sb('hbufC', [128, 4, D], F32)]
        gTs = [X.gT, sb('gTB', [128, 8, CH], BF16)]
        gb = sb('gb', [128, 8], F32)
        WinB = sb('WinB', [128, 8, 2048], BF16)
        Wg = [sb('Wrg', [128, 4, 2, 256], BF16), sb('Wig', [128, 4, 2, 256], BF16)]
        WoutB = sb('WoutB', [128, 8, D], BF16)
        cw = sb('cw', [128, 4, 8], F32)
        cb = sb('cb', [128, 8], F32)
        brg = sb('brg', [128, 8], F32)
        big = sb('big', [128, 8], F32)
        lam = sb('lam', [128, 8], F32)
        lt = [sb(f'lt{i}', [128, 8], F32) for i in range(4)]
        clam = sb('clam', [128, 8], F32)
        fng = sb('fng', [128, D], F32)
        ubuf = sb('ubuf', [128, 8, 4 + CH], F32)
        ucf = sb('ucf', [128, 8, CH], F32)
        ucb = sb('ucb', [128, 8, CH], BF16)
        rr = sb('rr', [128, 4, CH], F32)
        ii = sb('ii', [128, 4, CH], F32)
        mm_ = sb('mm', [128, 4, CH], F32)
        state = sb('state', [128, 8], F32)
        P = Prog(nc, st, 'p2')
        ps = X.ps
        stage = X.stage

        load_consts(P, X, A['ident'])
        vec = lambda ap: ap.rearrange('(k p) -> p k', p=128)
        P.dma(gb[:, :], vec(A['b_norm_g']), r=[], w=['gb'], slow=True)
        P.dma(cb[:, :], vec(A['b_conv_b']), r=[], w=['cb'], slow=True)
        P.dma(brg[:, :], vec(A['b_b_rg']), r=[], w=['brg'], slow=True)
        P.dma(big[:, :], vec(A['b_b_ig']), r=[], w=['big'], slow=True)
        P.dma(lam[:, :], vec(A['b_lam']), r=[], w=['lam'], slow=True)
        for j in range(4):
            P.dma(cw[:, j, :], vec(A['b_conv_w'][j, :]), r=[], w=['cw'], slow=True)
        P.dma(fng[:, :], A['fng_b'], r=[], w=['fng'])
        P.ts('dve', lt[3][:, :], lam[:, :], -1.0, None, ALU.mult, None, r=['lam'], w=['lt3'])
        P.tt('dve', lt[0][:, :], lam[:, :], lt[3][:, :], ALU.max, r=['lam', 'lt3'], w=['lt0'])
        P.act(lt[1][:, :], lt[0][:, :], AF.Exp, r=['lt0'], w=['lt1'], scale=-1.0)
        P.act(lt[2][:, :], lt[1][:, :], AF.Ln, r=['lt1'], w=['lt2'], bias=1.0)
        P.ts('dve', lt[3][:, :], lt[3][:, :], 0.0, None, ALU.max, None, r=['lt3'], w=['lt3'])
        P.tt('dve', lt[0][:, :], lt[3][:, :], lt[2][:, :], ALU.add, r=['lt3', 'lt2'], w=['lt0'])
        P.ts('dve', clam[:, :], lt[0][:, :], -8.0, None, ALU.mult, None, r=['lt0'], w=['clam'])
        P.op('pool', lambda e: e.memset(state[:, :], 0.0), writes=[f'state{cc}' for cc in range(8)])
        P.op('pool', lambda e: e.memset(ubuf[:, :, 0:4], 0.0), writes=[f'ubuf{cc}' for cc in range(8)])
        slots = []
        for bi in range(3):
            for hf in range(2):
                slots.append((hbufs[bi][:, 2 * hf:2 * hf + 2, :].rearrange('p a b -> p (a b)'),
                              [f'hb{bi}_{2 * hf}', f'hb{bi}_{2 * hf + 1}']))
        nslab = [0]

        def next_slot():
            i = nslab[0]
            nslab[0] += 1
            return slots[i % 6][0], slots[i % 6][1], ('dve' if i % 2 == 0 else 'act')

        for k in range(8):
            sl, sk, eng = next_slot()
            P.dma(sl[:, 0:2048], A['b_w_in'][k * 128:(k + 1) * 128, :], r=[], w=sk)
            P.cvt(eng, WinB[:, k, :], sl[:, 0:2048], gb[:, k:k + 1], None, False, r=sk + ['gb'], w=['WinB'])
        for gi, nm in enumerate(('b_w_rg', 'b_w_ig')):
            sl, sk, eng = next_slot()
            sv = sl[:, 0:2048].rearrange('p (g kc j) -> p g kc j', g=4, kc=2)
            for g in range(4):
                P.dma(sv[:, g, :, :], A[nm][g].rearrange('(kc p) j -> p kc j', p=128), r=[], w=sk)
            P.cvt(eng, Wg[gi][:, :, :, :], sv, None, None, False, r=sk, w=[f'Wg{gi}'])
        for k2 in range(4):
            sl, sk, eng = next_slot()
            P.dma(sl[:, 0:2048].rearrange('p (k n) -> p k n', k=2),
                  A['b_w_out'][k2 * 256:(k2 + 1) * 256, :].rearrange('(k p) n -> p k n', p=128), r=[], w=sk)
            P.cvt(eng, WoutB[:, 2 * k2:2 * k2 + 2, :], sl[:, 0:2048].rearrange('p (k n) -> p k n', k=2), None, None, False, r=sk, w=['WoutB'])

        chunks = chunk_list(nreal)

        def N(c):
            t0, nt, tiles = chunks[c]
            hb, hk = hbufs[c % 3], f'hb{c % 3}_'
            load_chunk(P, X, tiles, lambda j: A['h1'][t0 + 128 * j: t0 + 128 * j + tiles[j][1], :], hb=hb, hk=hk)
            norm_transpose(P, X, tiles, tiles[0][1], hb=hb, hk=hk)

        def PJ_u(c):
            t0, nt, tiles = chunks[c]
            bk = [0, 1, 6]
            for cc in range(8):
                b = bk[cc % 3]
                for k in range(8):
                    P.mm(ps[b][:, 0:nt], WinB[:, k, cc * 128:(cc + 1) * 128], X.hnT[:, k, 0:nt], k == 0, k == 7,
                         r=['WinB', 'hnT'], w=[f'ps{b}'])
                P.cp('act', ubuf[:, cc, 4:4 + nt], ps[b][:, 0:nt], r=[f'ps{b}'], w=[f'ubuf{cc}'])

        def PJ_g(c):
            t0, nt, tiles = chunks[c]
            gT, gk = gTs[c % 2], f'gT{c % 2}'
            bk = [2, 3, 6]
            for cc in range(8):
                b2 = bk[cc % 3]
                for k in range(8):
                    P.mm(ps[b2][:, 0:nt], WinB[:, k, D + cc * 128:D + (cc + 1) * 128], X.hnT[:, k, 0:nt], k == 0, k == 7,
                         r=['WinB', 'hnT'], w=[f'ps{b2}'])
                P.act(gT[:, cc, 0:nt], ps[b2][:, 0:nt], AF.Silu, r=[f'ps{b2}'], w=[gk])

        def CV(c, ccs):
            t0, nt, tiles = chunks[c]
            for cc in ccs:
                uk = f'ubuf{cc}'
                P.act(ucf[:, cc, 0:nt], ubuf[:, cc, 1:1 + nt], AF.Identity, r=[uk, 'cw', 'cb'], w=[f'ucf{cc}'],
                      scale=cw[:, 0, cc:cc + 1], bias=cb[:, cc:cc + 1])
                for j in (1, 2, 3):
                    P.stt(ucf[:, cc, 0:nt], ubuf[:, cc, 1 + j:1 + j + nt], cw[:, j, cc:cc + 1], ucf[:, cc, 0:nt], ALU.mult, ALU.add,
                          r=[uk, 'cw', f'ucf{cc}'], w=[f'ucf{cc}'])
                P.cp('pool', ucb[:, cc, 0:nt], ucf[:, cc, 0:nt], r=[f'ucf{cc}'], w=[f'ucb{cc}'])
                P.cp('pool', ubuf[:, cc, 0:4], ubuf[:, cc, nt:nt + 4], r=[uk], w=[uk])

        def G_A(c, hf):
            t0, nt, tiles = chunks[c]
            for cc in range(4 * hf, 4 * hf + 4):
                g, jc, l = cc // 2, cc % 2, cc % 4
                for gi in range(2):
                    b = 4 + gi
                    for kc in range(2):
                        P.mm(ps[b][:, 0:nt], Wg[gi][:, g, kc, jc * 128:(jc + 1) * 128], ucb[:, 2 * g + kc, 0:nt],
                             kc == 0, kc == 1, r=[f'Wg{gi}', f'ucb{2 * g + kc}'], w=[f'ps{b}'])
                P.act(rr[:, l, 0:nt], ps[4][:, 0:nt], AF.Sigmoid, r=['ps4', 'brg'], w=[f'rr{l}'], bias=brg[:, cc:cc + 1])
                P.act(ii[:, l, 0:nt], ps[5][:, 0:nt], AF.Sigmoid, r=['ps5', 'big'], w=[f'ii{l}'], bias=big[:, cc:cc + 1])

        def G_B1(c, hf):
            t0, nt, tiles = chunks[c]
            ccs = range(4 * hf, 4 * hf + 4)
            for cc in ccs:
                l = cc % 4
                P.act(rr[:, l, 0:nt], rr[:, l, 0:nt], AF.Exp, r=[f'rr{l}', 'clam'], w=[f'rr{l}'], scale=clam[:, cc:cc + 1])
                P.act(mm_[:, l, 0:nt], rr[:, l, 0:nt], AF.Square, r=[f'rr{l}'], w=[f'mm{l}'])
                P.tt('pool', ii[:, l, 0:nt], ii[:, l, 0:nt], ucf[:, cc, 0:nt], ALU.mult, r=[f'ii{l}', f'ucf{cc}'], w=[f'ii{l}'])
            for cc in ccs:
                l = cc % 4
                P.act(mm_[:, l, 0:nt], mm_[:, l, 0:nt], AF.Sqrt, r=[f'mm{l}'], w=[f'mm{l}'], scale=-1.0, bias=1.0)
                if c == 0:
                    P.op('pool', lambda e, l=l: e.memset(mm_[:, l, 0:1], 1.0), reads=[f'mm{l}'], writes=[f'mm{l}'])
                P.tt('dve', ii[:, l, 0:nt], ii[:, l, 0:nt], mm_[:, l, 0:nt], ALU.mult, r=[f'ii{l}', f'mm{l}'], w=[f'ii{l}'])

        def G_B2(c, hf):
            t0, nt, tiles = chunks[c]
            gT, gk = gTs[c % 2], f'gT{c % 2}'
            for cc in range(4 * hf, 4 * hf + 4):
                l = cc % 4
                P.op('dve', lambda e, l=l, cc=cc: e.tensor_tensor_scan(mm_[:, l, 0:nt], rr[:, l, 0:nt], ii[:, l, 0:nt],
                                                                        state[:, cc:cc + 1], ALU.mult, ALU.add),
                     reads=[f'rr{l}', f'ii{l}', f'state{cc}'], writes=[f'mm{l}'])
                P.cp('dve', state[:, cc:cc + 1], mm_[:, l, nt - 1:nt], r=[f'mm{l}'], w=[f'state{cc}'])
                P.tt('pool', gT[:, cc, 0:nt], mm_[:, l, 0:nt], gT[:, cc, 0:nt], ALU.mult, r=[f'mm{l}', gk], w=[gk])

        def O(c):
            t0, nt, tiles = chunks[c]
            p = c % 2
            hb, hk, gT, gk = hbufs[c % 3], f'hb{c % 3}_', gTs[p], f'gT{p}'
            wout_residual(P, X, tiles, WoutB, 'WoutB', hb=hb, hk=hk, gT=gT, gk=gk)
            if c > 0:
                for j, (off, tn) in enumerate(tiles):
                    P.act(X.tokbf[j % 2][0:tn, :], hb[0:tn, j, :], AF.Square, r=[f'{hk}{j}'], w=[f'tokbf{j % 2}', f'ssq{j}'],
                          accum_out=X.ssq[0:tn, j:j + 1])
                P.act(X.srt[:, 0:4], X.ssq[:, 0:4], AF.Sqrt, r=[f'ssq{j}' for j in range(4)] + ['epsc'], w=['srt'], scale=1.0 / D,
                      bias=X.epsc[:, 0:1])
                P.op('dve', lambda e: e.reciprocal(X.rstd[:, 0:4], X.srt[:, 0:4]), reads=['srt'], writes=['rstd'])
                for j, (off, tn) in enumerate(tiles):
                    P.stt(hb[0:tn, j, :], hb[0:tn, j, :], X.rstd[0:tn, j:j + 1], fng[0:tn, :], ALU.mult, ALU.mult,
                          r=[f'{hk}{j}', 'rstd', 'fng'], w=[f'{hk}{j}'])
                    r0 = CH * (c - 1) + off
                    P.dma(A['out'][r0:r0 + tn, :], hb[0:tn, j, :], r=[f'{hk}{j}'], w=[])

        n = len(chunks)
        N(0)
        PJ_u(0)
        PJ_g(0)
        CV(0, range(8))
        if n > 1:
            N(1)
        for c in range(n):
            if c + 1 < n:
                PJ_u(c + 1)
            if c >= 1:
                O(c - 1)
            G_A(c, 0)
            G_B1(c, 0)
            if c + 1 < n:
                PJ_g(c + 1)
            G_B2(c, 0)
            G_A(c, 1)
            G_B1(c, 1)
            if c + 2 < n:
                N(c + 2)
            G_B2(c, 1)
            if c + 1 < n:
                CV(c + 1, range(8))
        O(n - 1)
        P.emit()


def _consts(nreal):
    TT = NMETA + CH * nreal
    ident = np.eye(128, dtype=np.float32).astype(ml_dtypes.bfloat16)
    kk = np.arange(128)
    tri = (kk[:, None] <= kk[None, :]).astype(np.float32).astype(ml_dtypes.bfloat16)
    inv_freq = (np.float32(10000.0) ** (-(np.arange(0, ROPE, 2, dtype=np.float32) / np.float32(ROPE)))).astype(np.float32)
    pos = np.arange(TT, dtype=np.float32)
    ang = (pos[:, None] * inv_freq[None, :]).astype(np.float32)
    cos = np.cos(ang.astype(np.float64)).astype(np.float32).T
    sin = np.sin(ang.astype(np.float64)).astype(np.float32).T
    cs = np.ascontiguousarray(np.concatenate([cos, cos, sin, sin], axis=0))
    cos = np.ascontiguousarray(np.concatenate([cos, cos, cos, cos], axis=0))
    sin = np.ascontiguousarray(np.concatenate([sin, sin, sin, sin], axis=0))
    return ident, tri, cos, sin, cs


P1_IN = [('x', None, F32), ('meta', [NMETA, D], F32), ('a_norm_g', [D], F32), ('a_w_in', [D, 1728], F32),
         ('a_q_norm_g', [QL], F32), ('a_kv_norm_g', [KVL], F32), ('a_w_uq', [QL, 1536], F32),
         ('a_w_ukv', [KVL, 2048], F32), ('a_w_out', [D, D], F32), ('ident', [128, 128], BF16),
         ('tri', [128, 128], BF16), ('cos', None, F32), ('sin', None, F32), ('cs', None, F32)]
P2_IN = [('b_norm_g', [D], F32), ('b_w_in', [D, 2048], F32), ('b_conv_w', [4, D], F32), ('b_conv_b', [D], F32),
         ('b_w_rg', [4, 256, 256], F32), ('b_b_rg', [D], F32), ('b_w_ig', [4, 256, 256], F32), ('b_b_ig', [D], F32),
         ('b_lam', [D], F32), ('b_w_out', [D, D], F32), ('fng_b', [128, D], F32)]


def _declare(nc, specs, nreal):
    TT = NMETA + CH * nreal
    A = {}
    for name, shape, dt in specs:
        if name == 'x':
            shape = [CH * nreal, D]
        if name in ('cos', 'sin', 'cs'):
            shape = [128, TT]
        A[name] = nc.dram_tensor(name, shape, dt, kind='ExternalInput').ap()
    return A


def build(mode, nreal):
    nc = bass.Bass('TRN2', target_bir_lowering=False)
    nc._semstack = ExitStack()
    TT = NMETA + CH * nreal
    A = {}
    if mode in ('p1', 'fused'):
        A.update(_declare(nc, P1_IN, nreal))
    if mode in ('p2', 'fused'):
        A.update(_declare(nc, P2_IN, nreal))
        if 'ident' not in A:
            A['ident'] = nc.dram_tensor('ident', [128, 128], BF16, kind='ExternalInput').ap()
    if mode == 'p1':
        A['h1'] = nc.dram_tensor('h1', [TT, D], F32, kind='ExternalOutput').ap()
    elif mode == 'p2':
        A['h1'] = nc.dram_tensor('h1', [TT, D], F32, kind='ExternalInput').ap()
    else:
        A['h1'] = nc.dram_tensor('h1', [TT, D], F32).ap()
    if mode in ('p2', 'fused'):
        A['out'] = nc.dram_tensor('out', [CH * nreal, D], F32, kind='ExternalOutput').ap()
    if mode in ('p1', 'fused'):
        build_phase1(nc, A, nreal)
    if mode in ('p2', 'fused'):
        build_phase2(nc, A, nreal)
    return nc


def host_inputs(inp, nreal, b):
    ident, tri, cos, sin, cs = _consts(nreal)
    f = lambda a: np.ascontiguousarray(np.asarray(a, dtype=np.float32))
    m1 = {
        'x': f(inp['x'][b, :CH * nreal]), 'meta': f(inp['meta_tokens']), 'a_norm_g': f(inp['a_norm_g'][0]),
        'a_w_in': f(inp['a_w_in'][0]), 'a_q_norm_g': f(inp['a_q_norm_g'][0]), 'a_kv_norm_g': f(inp['a_kv_norm_g'][0]),
        'a_w_uq': f(inp['a_w_uq'][0]), 'a_w_ukv': f(inp['a_w_ukv'][0]), 'a_w_out': f(inp['a_w_out'][0]),
        'ident': ident, 'tri': tri, 'cos': cos, 'sin': sin, 'cs': cs,
    }
    m2 = {
        'b_norm_g': f(inp['b_norm_g'][0]), 'b_w_in': f(inp['b_w_in'][0]), 'b_conv_w': f(inp['b_conv_w'][0]),
        'b_conv_b': f(inp['b_conv_b'][0]), 'b_w_rg': f(inp['b_w_rg'][0]), 'b_b_rg': f(inp['b_b_rg'][0]),
        'b_w_ig': f(inp['b_w_ig'][0]), 'b_b_ig': f(inp['b_b_ig'][0]), 'b_lam': f(inp['b_lam'][0]),
        'b_w_out': f(inp['b_w_out'][0]),
        'fng_b': np.ascontiguousarray(np.broadcast_to(f(inp['final_norm_g'])[None, :], (128, D))),
        'ident': ident,
    }
    return m1, m2


MODE = 'fused'
NREAL = SEQ // CH


def kernel(**inputs):
    nreal = NREAL
    cores = list(range(NCORES))
    maps = [host_inputs(inputs, nreal, b) for b in cores]
    if MODE == 'fused':
        nc = build('fused', nreal)
        in_maps = [dict(m1, **m2) for (m1, m2) in maps]
        res = run_bass_kernel_spmd(nc, in_maps, core_ids=cores)
        out = np.stack([np.asarray(r['out']) for r in res.results], axis=0)
    else:
        nc1 = build('p1', nreal)
        res1 = run_bass_kernel_spmd(nc1, [m1 for (m1, m2) in maps], core_ids=cores)
        nc2 = build('p2', nreal)
        in2 = [dict(m2, h1=np.asarray(r['h1'])) for (m1, m2), r in zip(maps, res1.results)]
        res2 = run_bass_kernel_spmd(nc2, in2, core_ids=cores)
        out = np.stack([np.asarray(r['out']) for r in res2.results], axis=0)
    return out.astype(np.float32)
```

```python
import numpy as np
import ml_dtypes
from contextlib import ExitStack
import concourse.bass as bass
import concourse.mybir as mybir
from concourse.bass_utils import run_bass_kernel_spmd

F32 = mybir.dt.float32
BF16 = mybir.dt.bfloat16
AF = mybir.ActivationFunctionType
ALU = mybir.AluOpType

D = 1024
NMETA = 16
SEQ = 4096
CH = 512
EPS = 1e-6
NH = 8
QL = 384
KVL = 256
ROPE = 64
SCALE = float((128 + 64) ** -0.5)
NCORES = 8


class Prog:
    NDMA = 8

    def __init__(self, nc, stack, tag):
        stack = nc._semstack
        self.nc = nc
        self.tag = tag
        self.names = ['pe', 'act', 'dve', 'pool', 'sp']
        self.ops = {e: [] for e in self.names}
        self.sem = {}
        for e in ('pe', 'act', 'dve', 'pool'):
            self.sem[e] = (f'{tag}_s_{e}', stack.enter_context(nc.semaphore(f'{tag}_s_{e}')))
        self.dsem = {}
        self.dcnt = {}
        self.stack = stack
        self.lastw = {}
        self.readers = {}

    def _dsems(self, q):
        if q not in self.dsem:
            self.dsem[q] = [(f'{self.tag}_d_{q}{i}',
                             self.stack.enter_context(self.nc.semaphore(f'{self.tag}_d_{q}{i}')))
                            for i in range(self.NDMA)]
            self.dcnt[q] = 0
        return self.dsem[q]

    def op(self, eng, fn, reads=(), writes=(), dma=False):
        idx = len(self.ops[eng])
        deps = set()

        def add(prod, kind):
            if prod is None:
                return
            peng, pidx = prod
            if peng == eng and pidx == idx:
                return
            pop = self.ops[peng][pidx]
            if peng == eng and not pop['dma'] and not dma:
                if eng == 'pe':
                    return
            deps.add(prod)
            pop['signal'] = True

        for k in reads:
            add(self.lastw.get(k), 'RAW')
            if k.startswith('ps'):
                for r in self.readers.get(k, ()):
                    if r[0] != eng:
                        add(r, 'WAR')
        for k in writes:
            add(self.lastw.get(k), 'WAW')
            for r in self.readers.get(k, ()):
                add(r, 'WAR')
        for k in reads:
            if k not in writes:
                self.readers.setdefault(k, []).append((eng, idx))
        for k in writes:
            self.lastw[k] = (eng, idx)
            self.readers[k] = []
        o = dict(fn=fn, deps=deps, dma=dma, signal=False)
        if dma:
            sems = self._dsems(eng)
            i = self.dcnt[eng]
            self.dcnt[eng] += 1
            o['dsem'] = sems[i % self.NDMA]
            o['dval'] = 16 * (i // self.NDMA + 1)
            o['prev'] = (sems[i % self.NDMA], 16 * (i // self.NDMA)) if i >= self.NDMA else None
        self.ops[eng].append(o)

    def mm(self, out, lhsT, rhs, start, stop, r, w):
        self.op('pe', lambda e: e.matmul(out, lhsT, rhs, start=start, stop=stop), reads=r, writes=w)

    def tr(self, out, in_, ident, r, w):
        self.op('pe', lambda e: e.transpose(out, in_, ident), reads=r, writes=w)

    def act(self, out, in_, func, r, w, **kw):
        self.op('act', lambda e: e.activation(out, in_, func, **kw), reads=r, writes=w)

    def cp(self, eng, out, in_, r, w):
        if eng == 'act':
            self.op('act', lambda e: e.copy(out, in_), reads=r, writes=w)
        else:
            self.op(eng, lambda e: e.tensor_copy(out, in_), reads=r, writes=w)

    def tt(self, eng, out, in0, in1, op, r, w):
        self.op(eng, lambda e: e.tensor_tensor(out, in0, in1, op), reads=r, writes=w)

    def ts(self, eng, out, in0, s1, s2, op0, op1, r, w):
        if op1 is None:
            self.op(eng, lambda e: e.tensor_scalar(out, in0, s1, None, op0), reads=r, writes=w)
        else:
            self.op(eng, lambda e: e.tensor_scalar(out, in0, s1, s2, op0, op1), reads=r, writes=w)

    def cvt(self, eng, out, in_, g, gneg, neg, r, w):
        if eng == 'act':
            if g is None:
                self.op('act', lambda e: e.copy(out, in_), reads=r, writes=w)
            else:
                sc = gneg if neg else g
                self.op('act', lambda e: e.activation(out, in_, AF.Identity, scale=sc), reads=r, writes=w)
        else:
            if g is None:
                self.op(eng, lambda e: e.tensor_copy(out, in_), reads=r, writes=w)
            elif neg:
                self.op(eng, lambda e: e.tensor_scalar(out, in_, g, -1.0, ALU.mult, ALU.mult), reads=r, writes=w)
            else:
                self.op(eng, lambda e: e.tensor_scalar(out, in_, g, None, ALU.mult), reads=r, writes=w)

    def stt(self, out, in0, scalar, in1, op0, op1, r, w):
        self.op('dve', lambda e: e.scalar_tensor_tensor(out, in0, scalar, in1, op0, op1), reads=r, writes=w)

    def dma(self, out, in_, r, w, q='sp', slow=False):
        if slow:
            def f(e):
                with self.nc.allow_non_contiguous_dma(reason="tiny strided vector load"):
                    return e.dma_start(out=out, in_=in_)
            self.op(q, f, reads=r, writes=w, dma=True)
        else:
            self.op(q, lambda e: e.dma_start(out=out, in_=in_), reads=r, writes=w, dma=True)

    def emit(self):
        nc = self.nc
        for e in self.names:
            cnt = 0
            for o in self.ops[e]:
                if not o['dma'] and o['signal']:
                    cnt += 1
                    o['sigval'] = cnt

        def run(e, E):
            waited = {}
            for o in self.ops[e]:
                waits = []
                for (peng, pidx) in o['deps']:
                    pop = self.ops[peng][pidx]
                    if pop['dma']:
                        waits.append((pop['dsem'], pop['dval']))
                    else:
                        waits.append((self.sem[peng], pop['sigval']))
                if o['dma'] and o['prev'] is not None:
                    waits.append(o['prev'])
                for (nm, s), v in sorted(waits, key=lambda t: (t[0][0], t[1])):
                    if waited.get(nm, 0) >= v:
                        continue
                    E.wait_ge(s, v)
                    waited[nm] = v
                ins = o['fn'](E)
                if o['dma']:
                    ins.then_inc(o['dsem'][1], 16)
                elif o['signal']:
                    ins.then_inc(self.sem[e][1], 1)
            if e in self.dsem:
                n = self.dcnt[e]
                for i, (nm, s) in enumerate(self.dsem[e]):
                    k = (n - i + self.NDMA - 1) // self.NDMA if n > i else 0
                    if k > 0 and waited.get(nm, 0) < 16 * k:
                        E.wait_ge(s, 16 * k)

        with nc.Block() as block:
            if self.ops['pe']:
                @block.tensor
                def _(E):
                    run('pe', E)
            if self.ops['act']:
                @block.scalar
                def _(E):
                    run('act', E)
            if self.ops['dve']:
                @block.vector
                def _(E):
                    run('dve', E)
            if self.ops['pool']:
                @block.gpsimd
                def _(E):
                    run('pool', E)
            if self.ops['sp']:
                @block.sync
                def _(E):
                    run('sp', E)


def chunk_list(nreal):
    chunks = [(0, NMETA, [(0, NMETA)])]
    for c in range(nreal):
        chunks.append((NMETA + CH * c, CH, [(128 * j, 128) for j in range(CH // 128)]))
    return chunks


class Ctx:
    pass


def alloc_common(nc, st, X, tag, stage_cols=2048):
    sb = lambda name, shape, dt: st.enter_context(nc.sbuf_tensor(f'{tag}_{name}', shape, dt))
    X.ident = sb('ident', [128, 128], BF16)
    X.hbuf = sb('hbuf', [128, 4, D], F32)
    X.tokbf = [sb(f'tokbf{i}', [128, D], BF16) for i in range(2)]
    X.hnT = sb('hnT', [128, 8, CH], BF16)
    X.gT = sb('gT', [128, 8, CH], BF16)
    X.ssq = sb('ssq', [128, 4], F32)
    X.srt = sb('srt', [128, 4], F32)
    X.rstd = sb('rstd', [128, 4], F32)
    X.stage = sb('stage', [128, stage_cols], F32)
    X.ps = [st.enter_context(nc.psum_tensor(f'{tag}_psb{i}', [128, 512], F32)) for i in range(8)]
    X.ptr = [X.ps[7][:, :].bitcast(BF16).rearrange('p (a b) -> p a b', b=128)]
    X.sb = sb


def load_chunk(P, X, tiles, src_rows, hb=None, hk='hbuf'):
    hb = X.hbuf if hb is None else hb
    for j, (off, tn) in enumerate(tiles):
        P.dma(hb[0:tn, j, :], src_rows(j), r=[], w=[f'{hk}{j}'])


def norm_transpose(P, X, tiles, tnmax, hb=None, hk='hbuf'):
    hb = X.hbuf if hb is None else hb
    nt_ = len(tiles)
    for j, (off, tn) in enumerate(tiles):
        P.act(X.tokbf[j % 2][0:tn, :], hb[0:tn, j, :], AF.Square, r=[f'{hk}{j}'], w=[f'tokbf{j % 2}', f'ssq{j}'],
              accum_out=X.ssq[0:tn, j:j + 1])
    P.act(X.srt[0:tnmax, 0:nt_], X.ssq[0:tnmax, 0:nt_], AF.Sqrt, r=[f'ssq{j}' for j in range(nt_)] + ['epsc'],
          w=['srt'], scale=1.0 / D, bias=X.epsc[0:tnmax, 0:1])
    P.op('dve', lambda e: e.reciprocal(X.rstd[0:tnmax, 0:nt_], X.srt[0:tnmax, 0:nt_]), reads=['srt'], writes=['rstd'])
    for j, (off, tn) in enumerate(tiles):
        tb = X.tokbf[j % 2]
        tk = f'tokbf{j % 2}'
        P.ts('dve', tb[0:tn, :], hb[0:tn, j, :], X.rstd[0:tn, j:j + 1], None, ALU.mult, None,
             r=[f'{hk}{j}', 'rstd'], w=[tk])
        pt = X.ptr[0]
        pk = 'ps7'
        for k in range(8):
            P.tr(pt[:, k, 0:tn], tb[0:tn, k * 128:(k + 1) * 128], X.ident[0:tn, 0:tn], r=[tk, 'ident'], w=[pk])
        P.cp('act' if j % 2 else 'dve', X.hnT[:, :, off:off + tn], pt[:, :, 0:tn], r=[pk], w=['hnT'])


def wout_residual(P, X, tiles, W, wkey, hb=None, hk='hbuf', gT=None, gk='gT'):
    hb = X.hbuf if hb is None else hb
    gT = X.gT if gT is None else gT
    n = 0
    for j, (off, tn) in enumerate(tiles):
        for half in range(2):
            b = n % 2
            n += 1
            pk = f'ps{b}'
            for h in range(8):
                P.mm(X.ps[b][0:tn, :], gT[:, h, off:off + tn], W[:, h, half * 512:(half + 1) * 512],
                     h == 0, h == 7, r=[gk, wkey], w=[pk])
            P.tt('dve', hb[0:tn, j, half * 512:(half + 1) * 512], hb[0:tn, j, half * 512:(half + 1) * 512],
                 X.ps[b][0:tn, :], ALU.add, r=[pk, f'{hk}{j}'], w=[f'{hk}{j}'])


def load_consts(P, X, ident_d):
    P.dma(X.ident[:, :], ident_d, r=[], w=['ident'])
    P.op('pool', lambda e: e.memset(X.epsc[:, :], EPS), writes=['epsc'])


def build_phase1(nc, A, nreal):
    with ExitStack() as st:
        X = Ctx()
        alloc_common(nc, st, X, 'sA')
        sb = X.sb
        X.epsc = sb('epsc', [128, 1], F32)
        tri = sb('tri', [128, 128], BF16)
        ones_bf = sb('ones_bf', [128, 128], BF16)
        ones_f = sb('ones_f', [128, 128], F32)
        ga = sb('ga', [128, 8], F32)
        gq = sb('gq', [128, 3], F32)
        gkv = sb('gkv', [128, 2], F32)
        WinA = sb('WinA', [128, 8, 1984], BF16)
        Wuq = sb('Wuq', [128, 3, 8, 256], BF16)
        WukT = sb('WukT', [128, 8, 256], BF16)
        Wuv = sb('Wuv', [128, 2, 8, 128], BF16)
        WoutA = sb('WoutA', [128, 8, D], BF16)
        TT = NMETA + CH * nreal
        NKT = 1 + 4 * nreal
        ckvT = sb('ckvT', [128, 2, TT], BF16)
        ckv = sb('ckv', [128, NKT, 256], BF16)
        krT = sb('krT', [128, TT], BF16)
        sq = [sb(f'sq{i}', [128, CH], F32) for i in range(3)]
        sqk = [sb(f'sqk{i}', [128, CH], F32) for i in range(2)]
        rstdq = sb('rstdq', [128, CH], F32)
        rstdkv = sb('rstdkv', [128, CH], F32)
        stmp = sb('stmp', [128, CH], F32)
        qlatT = sb('qlatT', [128, 3, CH], BF16)
        cosb = sb('cosb', [128, CH], F32)
        sinb = sb('sinb', [128, CH], F32)
        csb = sb('csb', [128, CH], F32)
        csq = sb('csq', [128, CH], F32)
        rtmp = sb('rtmp', [128, CH], F32)
        qn = [sb(f'qn{i}', [128, CH], BF16) for i in range(2)]
        Qt = [sb(f'Qt{i}', [128, 2, CH], BF16) for i in range(2)]
        qrope = [sb(f'qrope{i}', [128, CH], BF16) for i in range(2)]
        pT = [sb(f'pT{i}', [128, CH], BF16) for i in range(4)]
        pTm = sb('pTm', [128, CH], BF16)
        olat = sb('olat', [128, 2, CH], BF16)
        rinv = sb('rinv', [128, CH], F32)
        ytmp = sb('ytmp', [128, CH], F32)
        racc = [sb(f'racc{i}', [128, CH], F32) for i in range(2)]
        P = Prog(nc, st, 'p1')
        ps = X.ps
        stage = X.stage

        load_consts(P, X, A['ident'])
        P.dma(tri[:, :], A['tri'], r=[], w=['tri'])
        P.op('pool', lambda e: e.memset(ones_bf[:, :], 1.0), writes=['ones_bf'])
        P.op('pool', lambda e: e.memset(ones_f[:, :], 1.0), writes=['ones_f'])
        P.op('pool', lambda e: e.memset(pTm[:, :], 0.0), writes=['pTm'])
        P.op('pool', lambda e: e.memset(ckvT[:, :, 0:128], 0.0), writes=['ckvT0', 'ckvT1'])
        P.op('pool', lambda e: e.memset(krT[:, 0:128], 0.0), writes=['krT0', 'krT1'])
        P.op('pool', lambda e: e.memset(ckv[:, 0, :], 0.0), writes=['ckv0'])
        P.dma(ga[:, :], A['a_norm_g'].rearrange('(k p) -> p k', p=128), r=[], w=['ga'], slow=True)
        P.dma(gq[:, :], A['a_q_norm_g'].rearrange('(k p) -> p k', p=128), r=[], w=['gq'], slow=True)
        P.dma(gkv[:, :], A['a_kv_norm_g'].rearrange('(k p) -> p k', p=128), r=[], w=['gkv'], slow=True)
        gan = sb('gan', [128, 8], F32)
        gqn = sb('gqn', [128, 3], F32)
        P.ts('dve', gan[:, :], ga[:, :], -1.0, None, ALU.mult, None, r=['ga'], w=['gan'])
        P.ts('dve', gqn[:, :], gq[:, :], -1.0, None, ALU.mult, None, r=['gq'], w=['gqn'])
        slots = [(stage[:, 0:2048], ['stage']),
                 (X.hbuf[:, 0:2, :].rearrange('p a b -> p (a b)'), ['hbuf0', 'hbuf1']),
                 (X.hbuf[:, 2:4, :].rearrange('p a b -> p (a b)'), ['hbuf2', 'hbuf3'])]
        nslab = [0]

        def next_slot():
            i = nslab[0]
            nslab[0] += 1
            return slots[i % 3][0], slots[i % 3][1], ('dve' if i % 2 == 0 else 'act')

        for k in range(8):
            sl, sk, eng = next_slot()
            P.dma(sl[:, 0:1728], A['a_w_in'][k * 128:(k + 1) * 128, :], r=[], w=sk)
            g, gn = ga[:, k:k + 1], gan[:, k:k + 1]
            rk_ = sk + ['ga', 'gan']
            P.cvt(eng, WinA[:, k, 0:1728], sl[:, 0:1728], g, gn, False, r=rk_, w=['WinA'])
            for dup in range(2):
                o = 1728 + 64 * dup
                P.cvt(eng, WinA[:, k, o:o + 64], sl[:, 640:704], g, gn, False, r=rk_, w=['WinA'])
                o = 1856 + 64 * dup
                P.cvt(eng, WinA[:, k, o:o + 32], sl[:, 672:704], g, gn, True, r=rk_, w=['WinA'])
                P.cvt(eng, WinA[:, k, o + 32:o + 64], sl[:, 640:672], g, gn, False, r=rk_, w=['WinA'])
        for k in range(3):
            sl, sk, eng = next_slot()
            P.dma(sl[:, 0:1536], A['a_w_uq'][k * 128:(k + 1) * 128, :], r=[], w=sk)
            g, gn = gq[:, k:k + 1], gqn[:, k:k + 1]
            rk_ = sk + ['gq', 'gqn']
            sv = sl[:, 0:1536].rearrange('p (h c) -> p h c', c=192)
            P.cvt(eng, Wuq[:, k, :, 0:192], sv, g, gn, False, r=rk_, w=['Wuq'])
            P.cvt(eng, Wuq[:, k, :, 192:224], sv[:, :, 160:192], g, gn, True, r=rk_, w=['Wuq'])
            P.cvt(eng, Wuq[:, k, :, 224:256], sv[:, :, 128:160], g, gn, False, r=rk_, w=['Wuq'])
        identf = sb('identf', [128, 128], F32)
        P.cp('dve', identf[:, :], X.ident[:, :], r=['ident'], w=['identf'])
        for lc in range(2):
            sl, sk, eng = next_slot()
            P.dma(sl[:, 0:2048], A['a_w_ukv'][lc * 128:(lc + 1) * 128, :], r=[], w=sk)
            sv = sl[:, 0:2048].rearrange('p (h c) -> p h c', c=256)
            P.cvt(eng, Wuv[:, lc, :, :], sv[:, :, 128:256], None, None, False, r=sk, w=['Wuv'])
            for h in range(NH):
                b = h % 2
                P.tr(ps[b][:, 0:128], sv[:, h, 0:128], identf[:, :], r=sk + ['identf'], w=[f'ps{b}'])
                P.cp('act', WukT[:, h, lc * 128:(lc + 1) * 128], ps[b][:, 0:128], r=[f'ps{b}'], w=['WukT'])
        for k2 in range(4):
            sl, sk, eng = next_slot()
            P.dma(sl[:, 0:2048].rearrange('p (k n) -> p k n', k=2),
                  A['a_w_out'][k2 * 256:(k2 + 1) * 256, :].rearrange('(k p) n -> p k n', p=128), r=[], w=sk)
            P.cvt(eng, WoutA[:, 2 * k2:2 * k2 + 2, :], sl[:, 0:2048].rearrange('p (k n) -> p k n', k=2), None, None, False, r=sk, w=['WoutA'])

        chunks = chunk_list(nreal)
        for c, (t0, nt, tiles) in enumerate(chunks):
            if c == 0:
                load_chunk(P, X, tiles, lambda j: A['meta'][0:NMETA, :])
                P.dma(cosb[:, 0:nt], A['cos'][:, t0:t0 + nt], r=[], w=['cosb'])
                P.dma(sinb[:, 0:nt], A['sin'][:, t0:t0 + nt], r=[], w=['sinb'])
                P.dma(csb[:, 0:nt], A['cs'][:, t0:t0 + nt], r=[], w=['csb'])
            norm_transpose(P, X, tiles, tiles[0][1])

            def proj(col0, M, b):
                for k in range(8):
                    P.mm(ps[b][0:M, 0:nt], WinA[:, k, col0:col0 + M], X.hnT[:, k, 0:nt], k == 0, k == 7,
                         r=['WinA', 'hnT'], w=[f'ps{b}'])

            for i in range(2):
                proj(QL + 128 * i, 128, 2 + i)
                P.act(sqk[i][:, 0:nt], ps[2 + i][:, 0:nt], AF.Square, r=[f'ps{2 + i}'], w=[f'sqk{i}'])
            for i in range(3):
                b = i % 2
                proj(128 * i, 128, b)
                P.act(sq[i][:, 0:nt], ps[b][:, 0:nt], AF.Square, r=[f'ps{b}'], w=[f'sq{i}'])
                P.cp('dve', qlatT[:, i, 0:nt], ps[b][:, 0:nt], r=[f'ps{b}'], w=['qlatT'])
            for i in range(2):
                P.mm(ps[4][:, 0:nt], ones_f[:, :], sqk[i][:, 0:nt], i == 0, i == 1, r=['ones_f', f'sqk{i}'], w=['ps4'])
            for i in range(3):
                P.mm(ps[5][:, 0:nt], ones_f[:, :], sq[i][:, 0:nt], i == 0, i == 2, r=['ones_f', f'sq{i}'], w=['ps5'])
            P.act(rstdkv[:, 0:nt], ps[4][:, 0:nt], AF.Ln, r=['ps4', 'epsc'], w=['rstdkv'], scale=1.0 / KVL, bias=X.epsc[:, 0:1])
            P.act(rstdkv[:, 0:nt], rstdkv[:, 0:nt], AF.Exp, r=['rstdkv'], w=['rstdkv'], scale=-0.5)
            P.act(rstdq[:, 0:nt], ps[5][:, 0:nt], AF.Ln, r=['ps5', 'epsc'], w=['rstdq'], scale=1.0 / QL, bias=X.epsc[:, 0:1])
            P.act(rstdq[:, 0:nt], rstdq[:, 0:nt], AF.Exp, r=['rstdq'], w=['rstdq'], scale=-0.5)

            def gate_groups(hs):
                for h in hs:
                    b = h % 2
                    proj(704 + 128 * h, 128, b)
                    P.act(X.gT[:, h, 0:nt], ps[b][:, 0:nt], AF.Silu, r=[f'ps{b}'], w=['gT'])

            gate_groups(range(0, 3))
            for i in range(2):
                P.stt(ckvT[:, i, t0:t0 + nt], ps[2 + i][:, 0:nt], gkv[:, i:i + 1], rstdkv[:, 0:nt], ALU.mult, ALU.mult,
                      r=[f'ps{2 + i}', 'gkv', 'rstdkv'], w=[f'ckvT{c}'])
            P.tt('pool', csq[:, 0:nt], csb[:, 0:nt], rstdq[:, 0:nt], ALU.mult, r=['csb', 'rstdq'], w=['csq'])
            proj(1728, 128, 0)
            proj(1856, 128, 1)
            P.tt('dve', rtmp[:, 0:nt], ps[0][:, 0:nt], cosb[:, 0:nt], ALU.mult, r=['ps0', 'cosb'], w=['rtmp'])
            P.tt('dve', stmp[:, 0:nt], ps[1][:, 0:nt], sinb[:, 0:nt], ALU.mult, r=['ps1', 'sinb'], w=['stmp'])
            P.tt('dve', krT[:, t0:t0 + nt], rtmp[:, 0:nt], stmp[:, 0:nt], ALU.add, r=['rtmp', 'stmp'], w=[f'krT{c}'])
            if c + 1 < len(chunks):
                nt0, nnt = chunks[c + 1][0], chunks[c + 1][1]
                P.dma(cosb[:, 0:nnt], A['cos'][:, nt0:nt0 + nnt], r=[], w=['cosb'])
                P.dma(sinb[:, 0:nnt], A['sin'][:, nt0:nt0 + nnt], r=[], w=['sinb'])
                P.dma(csb[:, 0:nnt], A['cs'][:, nt0:nt0 + nnt], r=[], w=['csb'])
            gate_groups(range(3, 6))
            for j, (off, tn) in enumerate(tiles):
                kt = 0 if c == 0 else 1 + 4 * (c - 1) + j
                pt = X.ptr[0]
                pk = 'ps7'
                for i in range(2):
                    P.tr(pt[0:tn, i, :], ckvT[:, i, t0 + off:t0 + off + tn], X.ident[:, :], r=[f'ckvT{c}', 'ident'], w=[pk])
                P.cp('act' if j % 2 else 'dve', ckv[0:tn, kt, :], pt[0:tn, 0:2, :], r=[pk], w=[f'ckv{kt}'])
            gate_groups(range(6, 8))

            units = []
            if c == 0:
                units.append((0, 0, NMETA, 0, True))
            else:
                units.append((0, 0, NMETA, 0, False))
                for cc in range(1, c):
                    for j in range(4):
                        units.append((1 + 4 * (cc - 1) + j, NMETA + CH * (cc - 1) + 128 * j, 128, 0, False))
                for j in range(4):
                    units.append((1 + 4 * (c - 1) + j, t0 + 128 * j, 128, 128 * j, True))

            SB = [2, 3, 7]

            def kchunk(k0):
                return 0 if k0 < NMETA else 1 + (k0 - NMETA) // CH

            ACC = [(4, 5), (6, 1)]

            def g_qn(h):
                s = h % 2
                for k in range(3):
                    P.mm(ps[0][:, 0:nt], Wuq[:, k, h, 0:128], qlatT[:, k, 0:nt], k == 0, k == 2, r=['Wuq', 'qlatT'], w=['ps0'])
                P.tt('dve', qn[s][:, 0:nt], ps[0][:, 0:nt], rstdq[:, 0:nt], ALU.mult, r=['ps0', 'rstdq'], w=[f'qn{s}'])

            def g_rope(h):
                s = h % 2
                for k in range(3):
                    P.mm(ps[0][:, 0:nt], Wuq[:, k, h, 128:256], qlatT[:, k, 0:nt], k == 0, k == 2, r=['Wuq', 'qlatT'], w=['ps0'])
                P.tt('dve', qrope[s][:, 0:nt], ps[0][:, 0:nt], csq[:, 0:nt], ALU.mult, r=['ps0', 'csq'], w=[f'qrope{s}'])

            def g_qt(h, lc):
                s = h % 2
                P.mm(ps[0][:, 0:nt], WukT[:, h, lc * 128:(lc + 1) * 128], qn[s][:, 0:nt], True, True, r=['WukT', f'qn{s}'], w=['ps0'])
                P.cp('dve', Qt[s][:, lc, 0:nt], ps[0][:, 0:nt], r=['ps0'], w=[f'Qt{s}'])

            def g_rowsum(h):
                P.mm(ps[0][:, 0:nt], ones_f[:, :], racc[h % 2][:, 0:nt], True, True, r=['ones_f', f'racc{h % 2}'], w=['ps0'])

            def g_rinv(h):
                P.act(rinv[:, 0:nt], ps[0][:, 0:nt], AF.Ln, r=['ps0'], w=['rinv'])
                P.act(rinv[:, 0:nt], rinv[:, 0:nt], AF.Exp, r=['rinv'], w=['rinv'], scale=-1.0)

            def g_y(h):
                for lc in range(2):
                    P.mm(ps[0][:, 0:nt], Wuv[:, lc, h, :], olat[:, lc, 0:nt], lc == 0, lc == 1, r=['Wuv', 'olat'], w=['ps0'])
                P.tt('dve', ytmp[:, 0:nt], ps[0][:, 0:nt], rinv[:, 0:nt], ALU.mult, r=['ps0', 'rinv'], w=['ytmp'])
                P.tt('pool', X.gT[:, h, 0:nt], ytmp[:, 0:nt], X.gT[:, h, 0:nt], ALU.mult, r=['ytmp', 'gT'], w=['gT'])

            def smm(h, u, ui):
                kt, k0, kn, qlo, diag = u
                s = h % 2
                b = SB[ui % 3]
                kc = kchunk(k0)
                km = 128
                rk = [f'ckvT{kc}', f'krT{kc}'] + (['ckvT1', 'krT1'] if kt == 0 else [])
                for lc in range(2):
                    P.mm(ps[b][0:km, qlo:nt], ckvT[:, lc, k0:k0 + km], Qt[s][:, lc, qlo:nt], lc == 0, False,
                         r=rk + [f'Qt{s}'], w=[f'ps{b}'])
                P.mm(ps[b][0:km, qlo:nt], krT[:, k0:k0 + km], qrope[s][:, qlo:nt], False, True,
                     r=rk + [f'qrope{s}'], w=[f'ps{b}'])

            def pv_exp(h, u, ui, nu):
                kt, k0, kn, qlo, diag = u
                b = SB[ui % 3]
                pb = pT[ui % 4]
                pk = f'pT{ui % 4}'
                if kt == 0:
                    pb, pk = pTm, 'pTm'
                P.act(pb[0:kn, qlo:nt], ps[b][0:kn, qlo:nt], AF.Exp, r=[f'ps{b}'], w=[pk], scale=SCALE)
                if diag:
                    P.tt('pool', pb[0:kn, qlo:qlo + kn], pb[0:kn, qlo:qlo + kn], tri[0:kn, 0:kn], ALU.mult, r=[pk, 'tri'], w=[pk])
                ra = racc[h % 2]
                rk = f'racc{h % 2}'
                if ui == 0:
                    P.op('pool', lambda e, ra=ra: e.memset(ra[:, :], 0.0), writes=[rk])
                P.tt('dve', ra[0:kn, qlo:nt], ra[0:kn, qlo:nt], pb[0:kn, qlo:nt], ALU.add, r=[rk, pk], w=[rk])

            def pv_mm(h, u, ui, nu):
                kt, k0, kn, qlo, diag = u
                pb = pT[ui % 4]
                pk = f'pT{ui % 4}'
                if kt == 0:
                    pb, pk = pTm, 'pTm'
                first = ui == 0
                last = ui == nu - 1
                for lc in range(2):
                    a = ACC[h % 2][lc]
                    P.mm(ps[a][:, qlo:nt], ckv[0:128, kt, lc * 128:(lc + 1) * 128], pb[0:128, qlo:nt], first, last,
                         r=[f'ckv{kt}', pk], w=[f'ps{a}'])

            def fin_evac(h):
                for lc in range(2):
                    a = ACC[h % 2][lc]
                    P.cp('dve', olat[:, lc, 0:nt], ps[a][:, 0:nt], r=[f'ps{a}'], w=['olat'])

            nu = len(units)
            for g in (g_qn, g_rope):
                g(0)
            g_qt(0, 0)
            g_qt(0, 1)
            for ui in range(min(2, nu)):
                smm(0, units[ui], ui)
            for h in range(NH):
                misc = []
                if h >= 1:
                    misc += [lambda h=h: g_rowsum(h - 1), lambda h=h: g_rinv(h - 1), lambda h=h: g_y(h - 1)]
                if h + 1 < NH:
                    misc += [lambda h=h: g_qn(h + 1), lambda h=h: g_rope(h + 1),
                             lambda h=h: g_qt(h + 1, 0), lambda h=h: g_qt(h + 1, 1)]
                for ui in range(nu):
                    if ui + 2 < nu:
                        smm(h, units[ui + 2], ui + 2)
                    pv_exp(h, units[ui], ui, nu)
                    pv_mm(h, units[ui], ui, nu)
                    take = 1 if nu - ui > len(misc) else (len(misc) if ui == nu - 1 else 2)
                    for _ in range(min(take, len(misc))):
                        misc.pop(0)()
                while misc:
                    misc.pop(0)()
                if h + 1 < NH:
                    for ui in range(min(2, nu)):
                        smm(h + 1, units[ui], ui)
                fin_evac(h)
            g_rowsum(NH - 1)
            g_rinv(NH - 1)
            g_y(NH - 1)

            wout_residual(P, X, tiles, WoutA, 'WoutA')
            for j, (off, tn) in enumerate(tiles):
                P.dma(A['h1'][t0 + off:t0 + off + tn, :], X.hbuf[0:tn, j, :], r=[f'hbuf{j}'], w=[])
                if c + 1 < len(chunks) and c > 0:
                    P.dma(X.hbuf[:, j, :], A['x'][CH * c + 128 * j: CH * c + 128 * (j + 1), :], r=[], w=[f'hbuf{j}'], q='act')
            if c == 0 and len(chunks) > 1:
                for j in range(4):
                    P.dma(X.hbuf[:, j, :], A['x'][128 * j: 128 * (j + 1), :], r=[], w=[f'hbuf{j}'])
        P.emit()


def build_phase2(nc, A, nreal):
    with ExitStack() as st:
        X = Ctx()
        alloc_common(nc, st, X, 'sB', stage_cols=1024)
        sb = X.sb
        X.epsc = sb('epsc', [128, 1], F32)
        hbufs = [X.hbuf, sb('hbufB', [128, 4, D], F32), sb('hbufC', [128, 4, D], F32)]
        gTs = [X.gT, sb('gTB', [128, 8, CH], BF16)]
        gb = sb('gb', [128, 8], F32)
        WinB = sb('WinB', [128, 8, 2048], BF16)
        Wg = [sb('Wrg', [128, 4, 2, 256], BF16), sb('Wig', [128, 4, 2, 256], BF16)]
        WoutB = sb('WoutB', [128, 8, D], BF16)
        cw = sb('cw', [128, 4, 8], F32)
        cb = sb('cb', [128, 8], F32)
        brg = sb('brg', [128, 8], F32)
        big = sb('big', [128, 8], F32)
        lam = sb('lam', [128, 8], F32)
        lt = [sb(f'lt{i}', [128, 8], F32) for i in range(4)]
        clam = sb('clam', [128, 8], F32)
        fng = sb('fng', [128, D], F32)
        ubuf = sb('ubuf', [128, 8, 4 + CH], F32)
        ucf = sb('ucf', [128, 8, CH], F32)
        ucb = sb('ucb', [128, 8, CH], BF16)
        rr = sb('rr', [128, 4, CH], F32)
        ii = sb('ii', [128, 4, CH], F32)
        mm_ = sb('mm', [128, 4, CH], F32)
        state = sb('state', [128, 8], F32)
        P = Prog(nc, st, 'p2')
        ps = X.ps
        stage = X.stage

        load_consts(P, X, A['ident'])
        vec = lambda ap: ap.rearrange('(k p) -> p k', p=128)
        P.dma(gb[:, :], vec(A['b_norm_g']), r=[], w=['gb'], slow=True)
        P.dma(cb[:, :], vec(A['b_conv_b']), r=[], w=['cb'], slow=True)
        P.dma(brg[:, :], vec(A['b_b_rg']), r=[], w=['brg'], slow=True)
        P.dma(big[:, :], vec(A['b_b_ig']), r=[], w=['big'], slow=True)
        P.dma(lam[:, :], vec(A['b_lam']), r=[], w=['lam'], slow=True)
        for j in range(4):
            P.dma(cw[:, j, :], vec(A['b_conv_w'][j, :]), r=[], w=['cw'], slow=True)
        P.dma(fng[:, :], A['fng_b'], r=[], w=['fng'])
        P.ts('dve', lt[3][:, :], lam[:, :], -1.0, None, ALU.mult, None, r=['lam'], w=['lt3'])
        P.tt('dve', lt[0][:, :], lam[:, :], lt[3][:, :], ALU.max, r=['lam', 'lt3'], w=['lt0'])
        P.act(lt[1][:, :], lt[0][:, :], AF.Exp, r=['lt0'], w=['lt1'], scale=-1.0)
        P.act(lt[2][:, :], lt[1][:, :], AF.Ln, r=['lt1'], w=['lt2'], bias=1.0)
        P.ts('dve', lt[3][:, :], lt[3][:, :], 0.0, None, ALU.max, None, r=['lt3'], w=['lt3'])
        P.tt('dve', lt[0][:, :], lt[3][:, :], lt[2][:, :], ALU.add, r=['lt3', 'lt2'], w=['lt0'])
        P.ts('dve', clam[:, :], lt[0][:, :], -8.0, None, ALU.mult, None, r=['lt0'], w=['clam'])
        P.op('pool', lambda e: e.memset(state[:, :], 0.0), writes=[f'state{cc}' for cc in range(8)])
        P.op('pool', lambda e: e.memset(ubuf[:, :, 0:4], 0.0), writes=[f'ubuf{cc}' for cc in range(8)])
        slots = []
        for bi in range(3):
            for hf in range(2):
                slots.append((hbufs[bi][:, 2 * hf:2 * hf + 2, :].rearrange('p a b -> p (a b)'),
                              [f'hb{bi}_{2 * hf}', f'hb{bi}_{2 * hf + 1}']))
        nslab = [0]

        def next_slot():
            i = nslab[0]
            nslab[0] += 1
            return slots[i % 6][0], slots[i % 6][1], ('dve' if i % 2 == 0 else 'act')

        for k in range(8):
            sl, sk, eng = next_slot()
            P.dma(sl[:, 0:2048], A['b_w_in'][k * 128:(k + 1) * 128, :], r=[], w=sk)
            P.cvt(eng, WinB[:, k, :], sl[:, 0:2048], gb[:, k:k + 1], None, False, r=sk + ['gb'], w=['WinB'])
        for gi, nm in enumerate(('b_w_rg', 'b_w_ig')):
            sl, sk, eng = next_slot()
            sv = sl[:, 0:2048].rearrange('p (g kc j) -> p g kc j', g=4, kc=2)
            for g in range(4):
                P.dma(sv[:, g, :, :], A[nm][g].rearrange('(kc p) j -> p kc j', p=128), r=[], w=sk)
            P.cvt(eng, Wg[gi][:, :, :, :], sv, None, None, False, r=sk, w=[f'Wg{gi}'])
        for k2 in range(4):
            sl, sk, eng = next_slot()
            P.dma(sl[:, 0:2048].rearrange('p (k n) -> p k n', k=2),
                  A['b_w_out'][k2 * 256:(k2 + 1) * 256, :].rearrange('(k p) n -> p k n', p=128), r=[], w=sk)
            P.cvt(eng, WoutB[:, 2 * k2:2 * k2 + 2, :], sl[:, 0:2048].rearrange('p (k n) -> p k n', k=2), None, None, False, r=sk, w=['WoutB'])

        chunks = chunk_list(nreal)

        def N(c):
            t0, nt, tiles = chunks[c]
            hb, hk = hbufs[c % 3], f'hb{c % 3}_'
            load_chunk(P, X, tiles, lambda j: A['h1'][t0 + 128 * j: t0 + 128 * j + tiles[j][1], :], hb=hb, hk=hk)
            norm_transpose(P, X, tiles, tiles[0][1], hb=hb, hk=hk)

        def PJ_u(c):
            t0, nt, tiles = chunks[c]
            bk = [0, 1, 6]
            for cc in range(8):
                b = bk[cc % 3]
                for k in range(8):
                    P.mm(ps[b][:, 0:nt], WinB[:, k, cc * 128:(cc + 1) * 128], X.hnT[:, k, 0:nt], k == 0, k == 7,
                         r=['WinB', 'hnT'], w=[f'ps{b}'])
                P.cp('act', ubuf[:, cc, 4:4 + nt], ps[b][:, 0:nt], r=[f'ps{b}'], w=[f'ubuf{cc}'])

        def PJ_g(c):
            t0, nt, tiles = chunks[c]
            gT, gk = gTs[c % 2], f'gT{c % 2}'
            bk = [2, 3, 6]
            for cc in range(8):
                b2 = bk[cc % 3]
                for k in range(8):
                    P.mm(ps[b2][:, 0:nt], WinB[:, k, D + cc * 128:D + (cc + 1) * 128], X.hnT[:, k, 0:nt], k == 0, k == 7,
                         r=['WinB', 'hnT'], w=[f'ps{b2}'])
                P.act(gT[:, cc, 0:nt], ps[b2][:, 0:nt], AF.Silu, r=[f'ps{b2}'], w=[gk])

        def CV(c, ccs):
            t0, nt, tiles = chunks[c]
            for cc in ccs:
                uk = f'ubuf{cc}'
                P.act(ucf[:, cc, 0:nt], ubuf[:, cc, 1:1 + nt], AF.Identity, r=[uk, 'cw', 'cb'], w=[f'ucf{cc}'],
                      scale=cw[:, 0, cc:cc + 1], bias=cb[:, cc:cc + 1])
                for j in (1, 2, 3):
                    P.stt(ucf[:, cc, 0:nt], ubuf[:, cc, 1 + j:1 + j + nt], cw[:, j, cc:cc + 1], ucf[:, cc, 0:nt], ALU.mult, ALU.add,
                          r=[uk, 'cw', f'ucf{cc}'], w=[f'ucf{cc}'])
                P.cp('pool', ubuf[:, cc, 0:4], ubuf[:, cc, nt:nt + 4], r=[uk], w=[uk])
                P.cp('pool', ucb[:, cc, 0:nt], ucf[:, cc, 0:nt], r=[f'ucf{cc}'], w=[f'ucb{cc}'])

        def G_A(c, hf):
            t0, nt, tiles = chunks[c]
            for cc in range(4 * hf, 4 * hf + 4):
                g, jc, l = cc // 2, cc % 2, cc % 4
                for gi in range(2):
                    b = 4 + gi
                    for kc in range(2):
                        P.mm(ps[b][:, 0:nt], Wg[gi][:, g, kc, jc * 128:(jc + 1) * 128], ucb[:, 2 * g + kc, 0:nt],
                             kc == 0, kc == 1, r=[f'Wg{gi}', f'ucb{2 * g + kc}'], w=[f'ps{b}'])
                P.act(rr[:, l, 0:nt], ps[4][:, 0:nt], AF.Sigmoid, r=['ps4', 'brg'], w=[f'rr{l}'], bias=brg[:, cc:cc + 1])
                P.act(ii[:, l, 0:nt], ps[5][:, 0:nt], AF.Sigmoid, r=['ps5', 'big'], w=[f'ii{l}'], bias=big[:, cc:cc + 1])

        def G_B1(c, hf):
            t0, nt, tiles = chunks[c]
            ccs = range(4 * hf, 4 * hf + 4)
            for cc in ccs:
                l = cc % 4
                P.act(rr[:, l, 0:nt], rr[:, l, 0:nt], AF.Exp, r=[f'rr{l}', 'clam'], w=[f'rr{l}'], scale=clam[:, cc:cc + 1])
                P.act(mm_[:, l, 0:nt], rr[:, l, 0:nt], AF.Square, r=[f'rr{l}'], w=[f'mm{l}'])
                P.tt('pool', ii[:, l, 0:nt], ii[:, l, 0:nt], ucf[:, cc, 0:nt], ALU.mult, r=[f'ii{l}', f'ucf{cc}'], w=[f'ii{l}'])
            for cc in ccs:
                l = cc % 4
                P.act(mm_[:, l, 0:nt], mm_[:, l, 0:nt], AF.Sqrt, r=[f'mm{l}'], w=[f'mm{l}'], scale=-1.0, bias=1.0)
                if c == 0:
                    P.op('pool', lambda e, l=l: e.memset(mm_[:, l, 0:1], 1.0), reads=[f'mm{l}'], writes=[f'mm{l}'])
                P.tt('dve', ii[:, l, 0:nt], ii[:, l, 0:nt], mm_[:, l, 0:nt], ALU.mult, r=[f'ii{l}', f'mm{l}'], w=[f'ii{l}'])

        def G_B2(c, hf):
            t0, nt, tiles = chunks[c]
            gT, gk = gTs[c % 2], f'gT{c % 2}'
            for cc in range(4 * hf, 4 * hf + 4):
                l = cc % 4
                P.op('dve', lambda e, l=l, cc=cc: e.tensor_tensor_scan(mm_[:, l, 0:nt], rr[:, l, 0:nt], ii[:, l, 0:nt],
                                                                        state[:, cc:cc + 1], ALU.mult, ALU.add),
                     reads=[f'rr{l}', f'ii{l}', f'state{cc}'], writes=[f'mm{l}'])
                P.cp('dve', state[:, cc:cc + 1], mm_[:, l, nt - 1:nt], r=[f'mm{l}'], w=[f'state{cc}'])
                P.tt('pool', gT[:, cc, 0:nt], mm_[:, l, 0:nt], gT[:, cc, 0:nt], ALU.mult, r=[f'mm{l}', gk], w=[gk])

        def O(c):
            t0, nt, tiles = chunks[c]
            p = c % 2
            hb, hk, gT, gk = hbufs[c % 3], f'hb{c % 3}_', gTs[p], f'gT{p}'
            wout_residual(P, X, tiles, WoutB, 'WoutB', hb=hb, hk=hk, gT=gT, gk=gk)
            if c > 0:
                for j, (off, tn) in enumerate(tiles):
                    P.act(X.tokbf[j % 2][0:tn, :], hb[0:tn, j, :], AF.Square, r=[f'{hk}{j}'], w=[f'tokbf{j % 2}', f'ssq{j}'],
                          accum_out=X.ssq[0:tn, j:j + 1])
                P.act(X.srt[:, 0:4], X.ssq[:, 0:4], AF.Sqrt, r=[f'ssq{j}' for j in range(4)] + ['epsc'], w=['srt'], scale=1.0 / D,
                      bias=X.epsc[:, 0:1])
                P.op('dve', lambda e: e.reciprocal(X.rstd[:, 0:4], X.srt[:, 0:4]), reads=['srt'], writes=['rstd'])
                for j, (off, tn) in enumerate(tiles):
                    P.stt(hb[0:tn, j, :], hb[0:tn, j, :], X.rstd[0:tn, j:j + 1], fng[0:tn, :], ALU.mult, ALU.mult,
                          r=[f'{hk}{j}', 'rstd', 'fng'], w=[f'{hk}{j}'])
                    r0 = CH * (c - 1) + off
                    P.dma(A['out'][r0:r0 + tn, :], hb[0:tn, j, :], r=[f'{hk}{j}'], w=[])

        n = len(chunks)
        N(0)
        PJ_u(0)
        PJ_g(0)
        CV(0, range(8))
        if n > 1:
            N(1)
        for c in range(n):
            if c + 1 < n:
                PJ_u(c + 1)
            if c >= 1:
                O(c - 1)
            G_A(c, 0)
            G_B1(c, 0)
            if c + 1 < n:
                PJ_g(c + 1)
            G_B2(c, 0)
            G_A(c, 1)
            G_B1(c, 1)
            if c + 2 < n:
                N(c + 2)
            G_B2(c, 1)
            if c + 1 < n:
                CV(c + 1, range(8))
        O(n - 1)
        P.emit()


def _consts(nreal):
    TT = NMETA + CH * nreal
    ident = np.eye(128, dtype=np.float32).astype(ml_dtypes.bfloat16)
    kk = np.arange(128)
    tri = (kk[:, None] <= kk[None, :]).astype(np.float32).astype(ml_dtypes.bfloat16)
    inv_freq = (np.float32(10000.0) ** (-(np.arange(0, ROPE, 2, dtype=np.float32) / np.float32(ROPE)))).astype(np.float32)
    pos = np.arange(TT, dtype=np.float32)
    ang = (pos[:, None] * inv_freq[None, :]).astype(np.float32)
    cos = np.cos(ang.astype(np.float64)).astype(np.float32).T
    sin = np.sin(ang.astype(np.float64)).astype(np.float32).T
    cs = np.ascontiguousarray(np.concatenate([cos, cos, sin, sin], axis=0))
    cos = np.ascontiguousarray(np.concatenate([cos, cos, cos, cos], axis=0))
    sin = np.ascontiguousarray(np.concatenate([sin, sin, sin, sin], axis=0))
    return ident, tri, cos, sin, cs


P1_IN = [('x', None, F32), ('meta', [NMETA, D], F32), ('a_norm_g', [D], F32), ('a_w_in', [D, 1728], F32),
         ('a_q_norm_g', [QL], F32), ('a_kv_norm_g', [KVL], F32), ('a_w_uq', [QL, 1536], F32),
         ('a_w_ukv', [KVL, 2048], F32), ('a_w_out', [D, D], F32), ('ident', [128, 128], BF16),
         ('tri', [128, 128], BF16), ('cos', None, F32), ('sin', None, F32), ('cs', None, F32)]
P2_IN = [('b_norm_g', [D], F32), ('b_w_in', [D, 2048], F32), ('b_conv_w', [4, D], F32), ('b_conv_b', [D], F32),
         ('b_w_rg', [4, 256, 256], F32), ('b_b_rg', [D], F32), ('b_w_ig', [4, 256, 256], F32), ('b_b_ig', [D], F32),
         ('b_lam', [D], F32), ('b_w_out', [D, D], F32), ('fng_b', [128, D], F32)]


def _declare(nc, specs, nreal):
    TT = NMETA + CH * nreal
    A = {}
    for name, shape, dt in specs:
        if name == 'x':
            shape = [CH * nreal, D]
        if name in ('cos', 'sin', 'cs'):
            shape = [128, TT]
        A[name] = nc.dram_tensor(name, shape, dt, kind='ExternalInput').ap()
    return A


def build(mode, nreal):
    nc = bass.Bass('TRN2', target_bir_lowering=False)
    nc._semstack = ExitStack()
    TT = NMETA + CH * nreal
    A = {}
    if mode in ('p1', 'fused'):
        A.update(_declare(nc, P1_IN, nreal))
    if mode in ('p2', 'fused'):
        A.update(_declare(nc, P2_IN, nreal))
        if 'ident' not in A:
            A['ident'] = nc.dram_tensor('ident', [128, 128], BF16, kind='ExternalInput').ap()
    if mode == 'p1':
        A['h1'] = nc.dram_tensor('h1', [TT, D], F32, kind='ExternalOutput').ap()
    elif mode == 'p2':
        A['h1'] = nc.dram_tensor('h1', [TT, D], F32, kind='ExternalInput').ap()
    else:
        A['h1'] = nc.dram_tensor('h1', [TT, D], F32).ap()
    if mode in ('p2', 'fused'):
        A['out'] = nc.dram_tensor('out', [CH * nreal, D], F32, kind='ExternalOutput').ap()
    if mode in ('p1', 'fused'):
        build_phase1(nc, A, nreal)
    if mode in ('p2', 'fused'):
        build_phase2(nc, A, nreal)
    return nc


def host_inputs(inp, nreal, b):
    ident, tri, cos, sin, cs = _consts(nreal)
    f = lambda a: np.ascontiguousarray(np.asarray(a, dtype=np.float32))
    m1 = {
        'x': f(inp['x'][b, :CH * nreal]), 'meta': f(inp['meta_tokens']), 'a_norm_g': f(inp['a_norm_g'][0]),
        'a_w_in': f(inp['a_w_in'][0]), 'a_q_norm_g': f(inp['a_q_norm_g'][0]), 'a_kv_norm_g': f(inp['a_kv_norm_g'][0]),
        'a_w_uq': f(inp['a_w_uq'][0]), 'a_w_ukv': f(inp['a_w_ukv'][0]), 'a_w_out': f(inp['a_w_out'][0]),
        'ident': ident, 'tri': tri, 'cos': cos, 'sin': sin, 'cs': cs,
    }
    m2 = {
        'b_norm_g': f(inp['b_norm_g'][0]), 'b_w_in': f(inp['b_w_in'][0]), 'b_conv_w': f(inp['b_conv_w'][0]),
        'b_conv_b': f(inp['b_conv_b'][0]), 'b_w_rg': f(inp['b_w_rg'][0]), 'b_b_rg': f(inp['b_b_rg'][0]),
        'b_w_ig': f(inp['b_w_ig'][0]), 'b_b_ig': f(inp['b_b_ig'][0]), 'b_lam': f(inp['b_lam'][0]),
        'b_w_out': f(inp['b_w_out'][0]),
        'fng_b': np.ascontiguousarray(np.broadcast_to(f(inp['final_norm_g'])[None, :], (128, D))),
        'ident': ident,
    }
    return m1, m2


MODE = 'fused'
NREAL = SEQ // CH


def kernel(**inputs):
    nreal = NREAL
    cores = list(range(NCORES))
    maps = [host_inputs(inputs, nreal, b) for b in cores]
    if MODE == 'fused':
        nc = build('fused', nreal)
        in_maps = [dict(m1, **m2) for (m1, m2) in maps]
        res = run_bass_kernel_spmd(nc, in_maps, core_ids=cores)
        out = np.stack([np.asarray(r['out']) for r in res.results], axis=0)
    else:
        nc1 = build('p1', nreal)
        res1 = run_bass_kernel_spmd(nc1, [m1 for (m1, m2) in maps], core_ids=cores)
        nc2 = build('p2', nreal)
        in2 = [dict(m2, h1=np.asarray(r['h1'])) for (m1, m2), r in zip(maps, res1.results)]
        res2 = run_bass_kernel_spmd(nc2, in2, core_ids=cores)
        out = np.stack([np.asarray(r['out']) for r in res2.results], axis=0)
    return out.astype(np.float32)
```
